# Optimizing a Trainium2 kernel written in Bass

```python
import math
import jax, jax.numpy as jnp
from jax import lax
import numpy as np

D_MODEL = 1024
BATCH = 8
SEQ = 4096
DEPTH = 4

N_MIXERS = 3
N_A_LAYERS = (DEPTH + 2) // 3
N_B_LAYERS = (DEPTH + 1) // 3
N_C_LAYERS = DEPTH // 3
EPS = 1e-6

SB_HEAD_DIM = 64
SB_HEADS = D_MODEL // SB_HEAD_DIM
Q_BLOCK = 128

GM_CHUNK = 128
GM_HALF = 2 * D_MODEL
GM_GROUPS = 16
GM_GROUP_DIM = GM_HALF // GM_GROUPS

SSM_INNER = 2 * D_MODEL
SSM_HEAD_DIM = 64
SSM_HEADS = SSM_INNER // SSM_HEAD_DIM
SSM_GROUPS = 8
SSM_HPG = SSM_HEADS // SSM_GROUPS
SSM_STATE = 128
SSM_CONV = 4
SSM_CHUNK = 128
SSM_CONV_DIM = SSM_INNER + 2 * SSM_GROUPS * SSM_STATE
SSM_PROJ = SSM_INNER + SSM_CONV_DIM + SSM_HEADS

FFN_HIDDEN = -(-8 * D_MODEL // (3 * 256)) * 256

kernel_name = "hybrid_sb_gmlp_ssd_trunk"


def rmsnorm(x, g):
    xf = x.astype(jnp.float32)
    y = xf * lax.rsqrt(jnp.mean(xf * xf, axis=-1, keepdims=True) + EPS)
    return (y * g.astype(jnp.float32)).astype(x.dtype)


def stick_breaking_attention(h, w_qkv, q_gain, k_gain, w_o):
    B_, S, _ = h.shape
    qkv = (h @ w_qkv).reshape(B_, S, 3, SB_HEADS, SB_HEAD_DIM)
    q = rmsnorm(qkv[:, :, 0], q_gain).transpose(0, 2, 1, 3)
    k = rmsnorm(qkv[:, :, 1], k_gain).transpose(0, 2, 1, 3)
    v = qkv[:, :, 2].transpose(0, 2, 1, 3)
    scale = 1.0 / math.sqrt(SB_HEAD_DIM)
    nb = S // Q_BLOCK
    q_blocks = q.reshape(B_, SB_HEADS, nb, Q_BLOCK, SB_HEAD_DIM).transpose(2, 0, 1, 3, 4)
    key_pos = jnp.arange(S)

    def one_block(args):
        qb, start = args
        t = start + jnp.arange(Q_BLOCK)
        z = jnp.einsum('bhtd,bhsd->bhts', qb, k).astype(jnp.float32) * scale
        mask = key_pos[None, :] < t[:, None]
        log_beta = jax.nn.log_sigmoid(z)
        log_1m = jnp.where(mask, jax.nn.log_sigmoid(-z), 0.0)
        suffix = lax.cumsum(log_1m, axis=3, reverse=True) - log_1m
        w = jnp.where(mask, jnp.exp(log_beta + suffix), 0.0)
        return jnp.einsum('bhts,bhsd->bhtd', w.astype(v.dtype), v)

    out = lax.map(one_block, (q_blocks, jnp.arange(nb) * Q_BLOCK))
    out = out.transpose(1, 0, 3, 2, 4).reshape(B_, S, SB_HEADS * SB_HEAD_DIM)
    return out @ w_o


def chunked_gmlp(h, w_in, b_in, v_gain, w_s, b_s, w_out):
    B_, S, _ = h.shape
    zz = jax.nn.gelu(h @ w_in + b_in, approximate=False)
    u, v = zz[..., :GM_HALF], zz[..., GM_HALF:]
    v = rmsnorm(v, v_gain)
    nc = S // GM_CHUNK
    v = v.reshape(B_, nc, GM_CHUNK, GM_GROUPS, GM_GROUP_DIM)
    causal = jnp.tril(jnp.ones((GM_CHUNK, GM_CHUNK), dtype=bool))
    w_s_c = jnp.where(causal, w_s, 0.0).astype(v.dtype)
    mixed = jnp.einsum('gts,bnsgc->bntgc', w_s_c, v) + b_s.T[None, None, :, :, None]
    return (u * mixed.reshape(B_, S, GM_HALF)) @ w_out


def ssd_chunked(x, dt, A, Bm, Cm):
    B_, S, _, _ = x.shape
    nc = S // SSM_CHUNK
    L = SSM_CHUNK
    f32 = jnp.float32
    xdt = (x.astype(f32) * dt[..., None]).reshape(B_, nc, L, SSM_GROUPS, SSM_HPG, SSM_HEAD_DIM)
    a = (dt * A).reshape(B_, nc, L, SSM_GROUPS, SSM_HPG)
    Bc = Bm.astype(f32).reshape(B_, nc, L, SSM_GROUPS, SSM_STATE)
    Cc = Cm.astype(f32).reshape(B_, nc, L, SSM_GROUPS, SSM_STATE)
    a_cum = jnp.cumsum(a, axis=2)

    a_t = a_cum.transpose(0, 1, 3, 4, 2)
    seg = a_t[..., :, None] - a_t[..., None, :]
    causal = jnp.tril(jnp.ones((L, L), dtype=bool))
    decay = jnp.exp(jnp.where(causal, seg, -jnp.inf))
    cb = jnp.einsum('bclgn,bcsgn->bcgls', Cc, Bc)
    y_diag = jnp.einsum('bcgrls,bcsgrp->bclgrp', cb[:, :, :, None] * decay, xdt)

    decay_to_end = jnp.exp(a_cum[:, :, -1:] - a_cum)
    states = jnp.einsum('bclgn,bclgrp->bcgrpn', Bc, xdt * decay_to_end[..., None])
    chunk_decay = jnp.exp(a_cum[:, :, -1])

    def step(hstate, inp):
        st, dec = inp
        return hstate * dec[..., None, None] + st, hstate

    init = jnp.zeros((B_, SSM_GROUPS, SSM_HPG, SSM_HEAD_DIM, SSM_STATE), f32)
    _, prev = lax.scan(step, init, (states.transpose(1, 0, 2, 3, 4, 5),
                                    chunk_decay.transpose(1, 0, 2, 3)))
    prev = prev.transpose(1, 0, 2, 3, 4, 5)

    y_off = jnp.einsum('bclgn,bcgrpn->bclgrp', Cc, prev) * jnp.exp(a_cum)[..., None]
    y = (y_diag + y_off).reshape(B_, S, SSM_HEADS, SSM_HEAD_DIM)
    return y.astype(x.dtype)


def mamba2_mixer(h, w_in, conv_w, conv_b, dt_bias, a_log, d_skip, norm_gain, w_out):
    B_, S, _ = h.shape
    zxbcdt = h @ w_in
    z = zxbcdt[..., :SSM_INNER]
    xbc = zxbcdt[..., SSM_INNER:SSM_INNER + SSM_CONV_DIM]
    dt = zxbcdt[..., SSM_INNER + SSM_CONV_DIM:]
    xbc = lax.conv_general_dilated(
        xbc, conv_w[:, None, :].astype(xbc.dtype), window_strides=(1,),
        padding=[(SSM_CONV - 1, 0)], dimension_numbers=('NWC', 'WIO', 'NWC'),
        feature_group_count=SSM_CONV_DIM) + conv_b
    xbc = jax.nn.silu(xbc)
    xs = xbc[..., :SSM_INNER].reshape(B_, S, SSM_HEADS, SSM_HEAD_DIM)
    Bm = xbc[..., SSM_INNER:SSM_INNER + SSM_GROUPS * SSM_STATE].reshape(B_, S, SSM_GROUPS, SSM_STATE)
    Cm = xbc[..., SSM_INNER + SSM_GROUPS * SSM_STATE:].reshape(B_, S, SSM_GROUPS, SSM_STATE)
    dt = jax.nn.softplus(dt.astype(jnp.float32) + dt_bias.astype(jnp.float32))
    A = -jnp.exp(a_log.astype(jnp.float32))
    y = ssd_chunked(xs, dt, A, Bm, Cm)
    y = (y + xs * d_skip[:, None]).reshape(B_, S, SSM_INNER)
    yg = (y * jax.nn.silu(z)).astype(jnp.float32).reshape(B_, S, SSM_GROUPS, SSM_INNER // SSM_GROUPS)
    yg = yg * lax.rsqrt(jnp.mean(yg * yg, axis=-1, keepdims=True) + EPS)
    y = (yg.reshape(B_, S, SSM_INNER) * norm_gain.astype(jnp.float32)).astype(h.dtype)
    return y @ w_out


def swiglu(h, w_gu, w_down):
    gu = h @ w_gu
    return (jax.nn.silu(gu[..., :FFN_HIDDEN]) * gu[..., FFN_HIDDEN:]) @ w_down


def setup_inputs(seed: int = 0) -> dict:
    key = jax.random.key(seed)
    ks = iter(jax.random.split(key, 40))

    def nrm(shape, scale):
        return jax.random.normal(next(ks), shape, jnp.float32) * scale

    def gain(shape):
        return 1.0 + nrm(shape, 0.02)

    D = D_MODEL
    x = nrm((BATCH, SEQ, D), 1.0)
    mix_norm = gain((DEPTH, D))
    ffn_norm = gain((DEPTH, D))
    sb_w_qkv = nrm((N_A_LAYERS, D, 3 * SB_HEADS * SB_HEAD_DIM), D ** -0.5)
    sb_q_gain = gain((N_A_LAYERS, SB_HEAD_DIM))
    sb_k_gain = gain((N_A_LAYERS, SB_HEAD_DIM))
    sb_w_o = nrm((N_A_LAYERS, SB_HEADS * SB_HEAD_DIM, D), (SB_HEADS * SB_HEAD_DIM) ** -0.5)
    gm_w_in = nrm((N_B_LAYERS, D, 2 * GM_HALF), D ** -0.5)
    gm_b_in = nrm((N_B_LAYERS, 2 * GM_HALF), 0.01)
    gm_v_gain = gain((N_B_LAYERS, GM_HALF))
    gm_w_s = nrm((N_B_LAYERS, GM_GROUPS, GM_CHUNK, GM_CHUNK), GM_CHUNK ** -0.5)
    gm_b_s = gain((N_B_LAYERS, GM_GROUPS, GM_CHUNK))
    gm_w_out = nrm((N_B_LAYERS, GM_HALF, D), GM_HALF ** -0.5)
    ssm_w_in = nrm((N_C_LAYERS, D, SSM_PROJ), D ** -0.5)
    ssm_conv_w = nrm((N_C_LAYERS, SSM_CONV, SSM_CONV_DIM), SSM_CONV ** -0.5)
    ssm_conv_b = nrm((N_C_LAYERS, SSM_CONV_DIM), 0.01)
    dt0 = jnp.exp(jax.random.uniform(next(ks), (N_C_LAYERS, SSM_HEADS), jnp.float32,
                                     math.log(1e-3), math.log(1e-1)))
    ssm_dt_bias = dt0 + jnp.log(-jnp.expm1(-dt0))
    ssm_a_log = jnp.log(jax.random.uniform(next(ks), (N_C_LAYERS, SSM_HEADS), jnp.float32, 1.0, 16.0))
    ssm_d = gain((N_C_LAYERS, SSM_HEADS))
    ssm_norm_gain = gain((N_C_LAYERS, SSM_INNER))
    ssm_w_out = nrm((N_C_LAYERS, SSM_INNER, D), SSM_INNER ** -0.5)
    ffn_w_gu = nrm((DEPTH, D, 2 * FFN_HIDDEN), D ** -0.5)
    ffn_w_down = nrm((DEPTH, FFN_HIDDEN, D), FFN_HIDDEN ** -0.5)
    return {
        "x": x, "mix_norm": mix_norm, "ffn_norm": ffn_norm,
        "sb_w_qkv": sb_w_qkv, "sb_q_gain": sb_q_gain, "sb_k_gain": sb_k_gain, "sb_w_o": sb_w_o,
        "gm_w_in": gm_w_in, "gm_b_in": gm_b_in, "gm_v_gain": gm_v_gain, "gm_w_s": gm_w_s,
        "gm_b_s": gm_b_s, "gm_w_out": gm_w_out,
        "ssm_w_in": ssm_w_in, "ssm_conv_w": ssm_conv_w, "ssm_conv_b": ssm_conv_b,
        "ssm_dt_bias": ssm_dt_bias, "ssm_a_log": ssm_a_log, "ssm_d": ssm_d,
        "ssm_norm_gain": ssm_norm_gain, "ssm_w_out": ssm_w_out,
        "ffn_w_gu": ffn_w_gu, "ffn_w_down": ffn_w_down,
    }


def reference(x, mix_norm, ffn_norm,
              sb_w_qkv, sb_q_gain, sb_k_gain, sb_w_o,
              gm_w_in, gm_b_in, gm_v_gain, gm_w_s, gm_b_s, gm_w_out,
              ssm_w_in, ssm_conv_w, ssm_conv_b, ssm_dt_bias, ssm_a_log, ssm_d,
              ssm_norm_gain, ssm_w_out,
              ffn_w_gu, ffn_w_down):
    for i in range(DEPTH):
        h = rmsnorm(x, mix_norm[i])
        kind = i % N_MIXERS
        j = i // N_MIXERS
        if kind == 0:
            m = stick_breaking_attention(h, sb_w_qkv[j], sb_q_gain[j], sb_k_gain[j], sb_w_o[j])
        elif kind == 1:
            m = chunked_gmlp(h, gm_w_in[j], gm_b_in[j], gm_v_gain[j], gm_w_s[j], gm_b_s[j], gm_w_out[j])
        else:
            m = mamba2_mixer(h, ssm_w_in[j], ssm_conv_w[j], ssm_conv_b[j], ssm_dt_bias[j],
                             ssm_a_log[j], ssm_d[j], ssm_norm_gain[j], ssm_w_out[j])
        x = x + m
        x = x + swiglu(rmsnorm(x, ffn_norm[i]), ffn_w_gu[i], ffn_w_down[i])
    return x
```

```python
import contextlib
import numpy as np
import concourse.bass as bass
import concourse.mybir as mybir
from concourse.bass_utils import run_bass_kernel_spmd

F32 = mybir.dt.float32
BF16 = mybir.dt.bfloat16
AF = mybir.ActivationFunctionType
ALU = mybir.AluOpType

PE, ACT, DVE, POOL, SP = "tensor", "scalar", "vector", "gpsimd", "sync"
ENGINES = [PE, ACT, DVE, POOL, SP]
SEM_EPOCH = 30000
N_DMA_SEMS = 6
SAME_ENGINE_SYNC = True

import os
CUT = 9
T = 4096
D = 1024
NT = T // 512
EPS = 1e-6
FFN_H = 2816
HC = FFN_H // 128


class Op:
    __slots__ = ("eng", "fn", "deps", "signal", "event", "vc", "is_dma", "waits", "is_mm")

    def __init__(self, eng, fn, is_dma=False, is_mm=False):
        self.eng = eng
        self.fn = fn
        self.deps = []
        self.signal = False
        self.event = None
        self.vc = None
        self.is_dma = is_dma
        self.is_mm = is_mm
        self.waits = []


class Prog:
    def __init__(self, nc):
        self.nc = nc
        self.ops = []
        self.last_w = {}
        self.readers = {}
        self.dma_count = {SP: 0, POOL: 0, ACT: 0}
        self.dma_prev = {}
        self.last_op = {}
        self.phase_dma = []
        self.bar = {}

    relax = False

    def op(self, eng, fn, reads=(), writes=(), is_dma=False, is_mm=False):
        o = Op(eng, fn, is_dma=is_dma, is_mm=is_mm)
        deps = []
        for r in reads:
            w = self.last_w.get(r)
            if w is not None:
                deps.append(w)
        for w_ in writes:
            w = self.last_w.get(w_)
            if w is not None:
                deps.append(w)
            rs = self.readers.get(w_)
            if rs:
                deps.extend(rs)
        if is_dma:
            k = self.dma_count[eng]
            self.dma_count[eng] = k + 1
            slot = (eng, k % N_DMA_SEMS)
            p = self.dma_prev.get(slot)
            if p is not None:
                deps.append(p)
            self.dma_prev[slot] = o
            o.event = slot
            self.phase_dma.append(o)
        b = self.bar.pop(eng, None)
        if b:
            deps.extend(b)
        seen = set()
        for d in deps:
            if d is o or id(d) in seen:
                continue
            seen.add(id(d))
            if (not is_dma) and (not d.is_dma) and d.eng == eng:
                if (o.is_mm and d.is_mm) or (self.relax and eng != POOL) or not SAME_ENGINE_SYNC:
                    continue
            o.deps.append(d)
        for r in reads:
            self.readers.setdefault(r, []).append(o)
        for w_ in writes:
            self.last_w[w_] = o
            self.readers[w_] = []
        self.ops.append(o)
        if not is_dma:
            self.last_op[eng] = o
        return o

    def barrier(self):
        deps = list(self.last_op.values()) + list(self.phase_dma)
        for b in self.bar.values():
            deps.extend(b)
        seen = set()
        dd = []
        for d in deps:
            if id(d) not in seen:
                seen.add(id(d))
                dd.append(d)
        deps = dd
        self.phase_dma = []
        self.bar = {e: list(deps) for e in ENGINES}
        self.last_w = {}
        self.readers = {}

    def mm(self, out, lhsT, rhs, start=True, stop=True, reads=(), writes=()):
        return self.op(PE, lambda e: e.matmul(out, lhsT, rhs, start=start, stop=stop),
                       reads, writes, is_mm=True)

    def transpose(self, out, in_, ident, reads=(), writes=()):
        return self.op(PE, lambda e: e.transpose(out, in_, ident), reads, writes, is_mm=True)

    def act(self, out, in_, func, reads=(), writes=(), **kw):
        return self.op(ACT, lambda e: e.activation(out, in_, func, **kw), reads, writes)

    def dma(self, eng, out, in_, reads=(), writes=(), **kw):
        return self.op(eng, lambda e: e.dma_start(out, in_, **kw), reads, writes, is_dma=True)

    def v(self, eng, name, *args, reads=(), writes=(), **kw):
        return self.op(eng, lambda e: getattr(e, name)(*args, **kw), reads, writes)

    def finalize(self):
        nc = self.nc
        for o in self.ops:
            for d in o.deps:
                d.signal = True
        cnt = {e: 0 for e in ENGINES}
        self.eng_sems = {e: [] for e in ENGINES}
        self.dma_sems = {}
        dma_val = {}
        for o in self.ops:
            if o.is_dma:
                slot = o.event
                if slot not in self.dma_sems:
                    self.dma_sems[slot] = nc.alloc_semaphore(name=f"dma_{slot[0]}_{slot[1]}")
                    dma_val[slot] = 0
                dma_val[slot] += 16
                o.event = (self.dma_sems[slot], dma_val[slot])
                o.signal = True
            elif o.signal:
                c = cnt[o.eng]
                ep = c // SEM_EPOCH
                if ep >= len(self.eng_sems[o.eng]):
                    self.eng_sems[o.eng].append(nc.alloc_semaphore(name=f"s_{o.eng}_{ep}"))
                cnt[o.eng] = c + 1
                o.event = (self.eng_sems[o.eng][ep], c - ep * SEM_EPOCH + 1)
        known = {e: {} for e in ENGINES}
        nwaits = 0
        for o in self.ops:
            kn = known[o.eng]
            w = {}
            for d in o.deps:
                s, v = d.event
                key = id(s)
                if kn.get(key, 0) >= v:
                    continue
                if key not in w or w[key][1] < v:
                    w[key] = (s, v)
            for d in o.deps:
                s, v = d.event
                if id(s) in w and d.vc is not None:
                    for k2, v2 in d.vc.items():
                        if kn.get(k2, 0) < v2:
                            kn[k2] = v2
            for key, (s, v) in w.items():
                if kn.get(key, 0) < v:
                    kn[key] = v
            o.waits = list(w.values())
            nwaits += len(o.waits)
            if o.signal:
                o.vc = dict(kn)
        self.nwaits = nwaits

    def emit(self, final_wait_ops=()):
        self.finalize()
        nc = self.nc
        by_eng = {e: [o for o in self.ops if o.eng == e] for e in ENGINES}
        finals = [o.event for o in final_wait_ops]

        def run(eng_name):
            def body(e):
                for o in by_eng[eng_name]:
                    for (s, v) in o.waits:
                        e.wait_ge(s, v)
                    inst = o.fn(e)
                    if o.signal:
                        s, v = o.event
                        inst.then_inc(s, 16 if o.is_dma else 1)
                if eng_name == SP:
                    mx = {}
                    for (s, v) in finals:
                        if id(s) not in mx or mx[id(s)][1] < v:
                            mx[id(s)] = (s, v)
                    for (s, v) in mx.values():
                        e.wait_ge(s, v)
            return body

        with nc.Block() as block:
            block.tensor(run(PE))
            block.scalar(run(ACT))
            block.vector(run(DVE))
            block.gpsimd(run(POOL))
            block.sync(run(SP))


def _colvec(v):
    return np.ascontiguousarray(np.asarray(v, np.float32).reshape(-1, 128).T)


COL = {}
ROW = {}


def _layout_tables():
    c = 0
    for i in range(4):
        COL[f"mixn{i}"] = (c, 8); c += 8
        COL[f"ffnn{i}"] = (c, 8); c += 8
    for j in range(2):
        COL[f"qg{j}"] = (c, 1); c += 1
        COL[f"kg{j}"] = (c, 1); c += 1
    COL["gm_bu"] = (c, 16); c += 16
    COL["gm_vg"] = (c, 16); c += 16
    COL["conv_w"] = (c, 128); c += 128
    COL["conv_b"] = (c, 32); c += 32
    COL["_n"] = c
    r = 0
    ROW["gm_bv"] = (r, 2048); r += 2048
    ROW["gm_bs"] = (r, 2048); r += 2048
    ROW["ssm_ng"] = (r, 2048); r += 2048
    ROW["dt_bias"] = (r, 32); r += 32
    ROW["a_log"] = (r, 32); r += 32
    ROW["d_skip"] = (r, 32); r += 32
    ROW["_n"] = r


_layout_tables()


def pack_small(inp):
    cols = np.zeros((128, COL["_n"]), np.float32)

    def put(name, arr):
        o, n = COL[name]
        cols[:, o:o + n] = arr

    for i in range(4):
        put(f"mixn{i}", _colvec(inp["mix_norm"][i]))
        put(f"ffnn{i}", _colvec(inp["ffn_norm"][i]))
    for j in range(2):
        put(f"qg{j}", np.tile(np.asarray(inp["sb_q_gain"][j], np.float32), 2)[:, None])
        put(f"kg{j}", np.tile(np.asarray(inp["sb_k_gain"][j], np.float32), 2)[:, None])
    put("gm_bu", _colvec(inp["gm_b_in"][0][:2048]))
    put("gm_vg", _colvec(inp["gm_v_gain"][0]))
    cw = np.asarray(inp["ssm_conv_w"][0], np.float32)
    put("conv_w", np.ascontiguousarray(cw.T.reshape(32, 128, 4).transpose(1, 0, 2)).reshape(128, 128))
    put("conv_b", _colvec(inp["ssm_conv_b"][0]))
    rows = np.zeros((1, ROW["_n"]), np.float32)

    def putr(name, arr):
        o, n = ROW[name]
        rows[0, o:o + n] = np.asarray(arr, np.float32).reshape(-1)

    putr("gm_bv", inp["gm_b_in"][0][2048:])
    putr("gm_bs", inp["gm_b_s"][0])
    putr("ssm_ng", inp["ssm_norm_gain"][0])
    putr("dt_bias", inp["ssm_dt_bias"][0])
    putr("a_log", inp["ssm_a_log"][0])
    putr("d_skip", inp["ssm_d"][0])
    wsT = np.ascontiguousarray(np.asarray(inp["gm_w_s"][0], np.float32).transpose(2, 0, 1))
    return cols, rows, wsT


class Builder:
    def __init__(self, layers=(0, 1, 2, 3), debug=False, plan=None, attn_pairs=8):
        self.attn_pairs = attn_pairs
        self.layers = list(layers)
        self.plan = plan
        self.debug = debug
        self.nc = bass.Bass("TRN2", target_bir_lowering=False)
        self.P = Prog(self.nc)
        self.uid = 0

    def dram_in(self, name, shape, dt=F32):
        return self.nc.dram_tensor(name, list(shape), dt, kind="ExternalInput").ap()

    def dram_scr(self, name, shape, dt):
        kind = "ExternalOutput" if self.debug else "Internal"
        return self.nc.dram_tensor(name, list(shape), dt, kind=kind).ap()

    def sb(self, stack, name, shape, dt):
        self.uid += 1
        return stack.enter_context(self.nc.sbuf_tensor(f"{name}_{self.uid}", list(shape), dt))

    def ps(self, stack, name, shape=(128, 512), dt=F32):
        self.uid += 1
        return stack.enter_context(self.nc.psum_tensor(f"{name}_{self.uid}", list(shape), dt))

    @contextlib.contextmanager
    def phase(self):
        with contextlib.ExitStack() as st:
            yield st
            self.P.barrier()

    def load_w(self, wsb, key, w_ap, c0, c1, KC):
        for kc in range(KC):
            self.P.dma(POOL, wsb[:, kc, :], w_ap[kc * 128:(kc + 1) * 128, c0:c1], writes=[(key, kc)])

    def declare_io(self):
        self.xT_in = self.dram_in("xT", [D, T])
        self.yT_out = self.nc.dram_tensor("yT", [D, T], F32, kind="ExternalOutput").ap()
        self.cols_in = self.dram_in("cols", [128, COL["_n"]])
        self.rows_in = self.dram_in("rows", [1, ROW["_n"]])
        self.wsT_in = self.dram_in("wsT", [128, 16, 128])
        W = {}
        W["sb_w_qkv"] = self.dram_in("sb_w_qkv", [2, 1024, 3072])
        W["sb_w_o"] = self.dram_in("sb_w_o", [2, 1024, 1024])
        W["gm_w_in"] = self.dram_in("gm_w_in", [1, 1024, 4096])
        W["gm_w_out"] = self.dram_in("gm_w_out", [1, 2048, 1024])
        W["ssm_w_in"] = self.dram_in("ssm_w_in", [1, 1024, 6176])
        W["ssm_w_out"] = self.dram_in("ssm_w_out", [1, 2048, 1024])
        W["ffn_w_gu"] = self.dram_in("ffn_w_gu", [4, 1024, 5632])
        W["ffn_w_down"] = self.dram_in("ffn_w_down", [4, 2816, 1024])
        self.W = W
        self.xs = self.dram_scr("xs", [D, T], F32)
        self.hT = self.dram_scr("hT", [D, T], BF16)
        self.actT = self.dram_scr("actT", [FFN_H, T], BF16)
        self.qT = self.dram_scr("qT", [D, T], BF16)
        self.kT = self.dram_scr("kT", [D, T], BF16)
        self.vtok = self.dram_scr("vtok", [T, D], BF16)
        self.oT = self.dram_scr("oT", [D, T], BF16)
        self.uT = self.dram_scr("uT", [2048, T], BF16)
        self.prodT = self.dram_scr("prodT", [2048, T], BF16)
        self.xbcT = self.dram_scr("xbcT", [4096, T], BF16)
        self.ynT = self.dram_scr("ynT", [2048, T], BF16)

    def setup_consts(self, st):
        nc, P = self.nc, self.P
        sb = lambda n, s, d: self.sb(st, n, s, d)
        self.cols = sb("cols", [128, COL["_n"]], F32)
        P.dma(SP, self.cols[:], self.cols_in, writes=["cols"])
        self.ident = sb("ident", [128, 128], BF16)
        self.ones_bf = sb("ones_bf", [128, 128], BF16)
        self.ones_f = sb("ones_f", [128, 128], F32)
        self.blk_f = sb("blk_f", [128, 128], F32)
        self.blk_bf = sb("blk_bf", [128, 128], BF16)
        self.triU_f = sb("triU_f", [128, 128], F32)
        self.tri_bf = sb("tri_bf", [128, 128], BF16)
        self.cmp_bf = sb("cmp_bf", [128, 128], BF16)
        self.maskLS = sb("maskLS", [128, 128], F32)
        self.triU_bf = self.cmp_bf
        self.m01 = sb("m01", [128, 4, 512], BF16)
        self.mneg = sb("mneg", [128, 4, 512], BF16)
        g = POOL
        P.v(g, "memset", self.ident[:], 0.0, writes=["c_ident"])
        P.op(g, lambda e: e.affine_select(self.ident[:], self.ident[:], [[-1, 128]], ALU.not_equal, 1.0,
                                          base=0, channel_multiplier=1), writes=["c_ident"], reads=["c_ident"])
        P.v(g, "memset", self.ones_bf[:], 1.0, writes=["c_ones_bf"])
        P.v(g, "memset", self.ones_f[:], 1.0, writes=["c_ones_f"])
        P.v(g, "memset", self.blk_f[:], 0.0, writes=["c_blk"])
        P.v(g, "memset", self.blk_f[0:64, 0:64], 1.0, writes=["c_blk"], reads=["c_blk"])
        P.v(g, "memset", self.blk_f[64:128, 64:128], 1.0, writes=["c_blk"], reads=["c_blk"])
        P.v(g, "memset", self.blk_bf[:], 0.0, writes=["c_blkb"])
        P.v(g, "memset", self.blk_bf[0:64, 0:64], 1.0, writes=["c_blkb"], reads=["c_blkb"])
        P.v(g, "memset", self.blk_bf[64:128, 64:128], 1.0, writes=["c_blkb"], reads=["c_blkb"])
        P.v(g, "memset", self.triU_f[:], 1.0, writes=["c_triU"])
        P.op(g, lambda e: e.affine_select(self.triU_f[:], self.triU_f[:], [[1, 128]], ALU.is_ge, 0.0,
                                          base=0, channel_multiplier=-1), writes=["c_triU"], reads=["c_triU"])
        P.v(g, "memset", self.maskLS[:], 1.0, writes=["c_maskLS"])
        P.op(g, lambda e: e.affine_select(self.maskLS[:], self.maskLS[:], [[1, 128]], ALU.is_ge, 0.0,
                                          base=0, channel_multiplier=-1), writes=["c_maskLS"], reads=["c_maskLS"])
        P.v(g, "memset", self.tri_bf[:], 1.0, writes=["c_tri"])
        P.op(g, lambda e: e.affine_select(self.tri_bf[:], self.tri_bf[:], [[-1, 128]], ALU.is_gt, 0.0,
                                          base=0, channel_multiplier=1), writes=["c_tri"], reads=["c_tri"])
        P.v(g, "memset", self.cmp_bf[:], 1.0, writes=["c_cmp"])
        P.op(g, lambda e: e.affine_select(self.cmp_bf[:], self.cmp_bf[:], [[1, 128]], ALU.is_ge, 0.0,
                                          base=0, channel_multiplier=-1), writes=["c_cmp"], reads=["c_cmp"])
        for m in range(4):
            P.v(g, "memset", self.m01[:, m, :], 1.0, writes=[("c_m01", m)])
            P.op(g, lambda e, m=m: e.affine_select(self.m01[:, m, :], self.m01[:, m, :], [[1, 512]], ALU.is_gt, 0.0,
                                                   base=-128 * m, channel_multiplier=-1),
                 writes=[("c_m01", m)], reads=[("c_m01", m)])
            P.v(g, "tensor_scalar", self.mneg[:, m, :], self.m01[:, m, :], -1.0, None, ALU.mult,
                reads=[("c_m01", m)], writes=[("c_mneg", m)])
        P.barrier()

    def col(self, name, i=0, n=1):
        o, _ = COL[name]
        return self.cols[:, o + i:o + i + n]

    def row_bc(self, name):
        o, n = ROW[name]
        return self.rows_in[0:1, o:o + n].broadcast_to([128, n])

    def norm_phase(self, src, gname):
        P = self.P
        with self.phase() as st:
            xt = [self.sb(st, "xt", [128, 8, 512], F32) for _ in range(2)]
            sq = [self.sb(st, "sq", [128, 8, 512], BF16) for _ in range(2)]
            ht = [self.sb(st, "ht", [128, 8, 512], BF16) for _ in range(2)]
            rs = [self.sb(st, "rs", [128, 512], F32) for _ in range(2)]
            pss = [self.ps(st, "pss") for _ in range(2)]
            for tt in range(NT):
                b = tt % 2
                ts = slice(tt * 512, (tt + 1) * 512)
                P.dma(SP, xt[b][:], src[:, ts].rearrange("(kc p) t -> p kc t", p=128), writes=[("xt", b)])
                P.act(sq[b][:], xt[b][:], AF.Square, reads=[("xt", b)], writes=[("sq", b)])
                for kc in range(8):
                    P.mm(pss[b][:], self.ones_bf[:], sq[b][:, kc, :], start=(kc == 0), stop=(kc == 7),
                         reads=[("sq", b)], writes=[("pss", b)])
                P.act(rs[b][:], pss[b][:], AF.Ln, bias=EPS, scale=1.0 / D, reads=[("pss", b)], writes=[("rs", b)])
                P.act(rs[b][:], rs[b][:], AF.Exp, scale=-0.5, reads=[("rs", b)], writes=[("rs", b)])
                for kc in range(8):
                    P.v(DVE, "scalar_tensor_tensor", ht[b][:, kc, :], xt[b][:, kc, :], self.col(gname, kc), rs[b][:],
                        ALU.mult, ALU.mult, reads=[("xt", b), ("rs", b)], writes=[("ht", b, kc)])
                P.dma(SP, self.hT[:, ts].rearrange("(kc p) t -> p kc t", p=128), ht[b][:],
                      reads=[("ht", b, kc) for kc in range(8)], writes=[("hT_d", tt)])

    def gemm_fm(self, in_d, KC, w_ap, col_groups, epilogue, pre_tile=None, post_tile=None, extra=None, npsum=6,
                norm=None, at_bufs=2):
        P = self.P
        allc = sorted({c for g in col_groups for c in g})
        cpos = {c: i for i, c in enumerate(allc)}
        with self.phase() as st:
            wsb = self.sb(st, "wsb", [128, KC, len(allc) * 128], BF16)
            nblk = 4 if len(col_groups) >= 8 else 1
            gpb = -(-len(col_groups) // nblk)
            col_key = {}
            for bi in range(0, len(col_groups), gpb):
                cs = sorted({c for g in col_groups[bi:bi + gpb] for c in g if c not in col_key})
                runs = []
                for c in cs:
                    if runs and runs[-1][1] == c:
                        runs[-1][1] = c + 128
                    else:
                        runs.append([c, c + 128])
                for (c0, c1) in runs:
                    assert cpos[c1 - 128] - cpos[c0] == (c1 - c0) // 128 - 1
                    p0 = cpos[c0] * 128
                    for kc in range(KC):
                        P.dma(POOL, wsb[:, kc, p0:p0 + (c1 - c0)], w_ap[kc * 128:(kc + 1) * 128, c0:c1],
                              writes=[("wsb", kc, c0)])
                    for c in range(c0, c1, 128):
                        col_key[c] = c0
            it = [self.sb(st, "it", [128, KC, 512], BF16) for _ in range(2)]
            pst = [self.ps(st, "pst") for _ in range(npsum)]
            if norm:
                n_src, n_g, n_store = norm
                nxt = self.sb(st, "nxt", [128, 8, 512], F32)
                nsq = self.sb(st, "nsq", [128, 8, 512], BF16)
                nrs = [self.sb(st, "nrs", [128, 512], F32) for _ in range(2)]
                npss = self.ps(st, "npss")
            ctx = extra(st) if extra else None
            pi = 0

            def prepare(tt):
                b = tt % 2
                ts = slice(tt * 512, (tt + 1) * 512)
                if norm:
                    P.dma(POOL, nxt[:], n_src[:, ts].rearrange("(kc p) t -> p kc t", p=128),
                          reads=[("xs_d", tt)], writes=["nxt"])
                    P.act(nsq[:], nxt[:], AF.Square, reads=["nxt"], writes=["nsq"])
                    for kc in range(8):
                        P.mm(npss[:], self.ones_bf[:], nsq[:, kc, :], start=(kc == 0), stop=(kc == 7),
                             reads=["nsq"], writes=["npss"])
                    P.act(nrs[b][:], npss[:], AF.Ln, bias=EPS, scale=1.0 / D, reads=["npss"], writes=[("nrs", b)])
                    P.act(nrs[b][:], nrs[b][:], AF.Exp, scale=-0.5, reads=[("nrs", b)], writes=[("nrs", b)])
                    for kc in range(8):
                        P.v(DVE, "scalar_tensor_tensor", it[b][:, kc, :], nxt[:, kc, :], self.col(n_g, kc),
                            nrs[b][:], ALU.mult, ALU.mult, reads=["nxt", ("nrs", b)], writes=[("it", b, kc)])
                    if n_store:
                        P.dma(SP, self.hT[:, ts].rearrange("(kc p) t -> p kc t", p=128), it[b][:],
                              reads=[("it", b, kc) for kc in range(8)], writes=[("hT_d", tt)])
                else:
                    P.dma(SP, it[b][:], in_d[:, ts].rearrange("(kc p) t -> p kc t", p=128),
                          reads=[("in_d", tt)], writes=[("it", b, kc) for kc in range(KC)])
                if pre_tile:
                    pre_tile(ctx, tt, ts)

            prepare(0)
            for tt in range(NT):
                b = tt % 2
                ts = slice(tt * 512, (tt + 1) * 512)
                if tt + 1 < NT:
                    prepare(tt + 1)
                for gi, g in enumerate(col_groups):
                    tiles, keys = [], []
                    for c in g:
                        pt = pst[pi % npsum]
                        key = ("pst", pi % npsum)
                        pi += 1
                        for kc in range(KC):
                            P.mm(pt[:], wsb[:, kc, cpos[c] * 128:(cpos[c] + 1) * 128], it[b][:, kc, :],
                                 start=(kc == 0), stop=(kc == KC - 1),
                                 reads=[("it", b, kc)] + ([("wsb", kc, col_key[c])] if (tt == 0) else []), writes=[key])
                        tiles.append(pt)
                        keys.append(key)
                    epilogue(ctx, tt, ts, gi, tiles, keys)
                if post_tile:
                    post_tile(ctx, tt, ts)

    def resid_gemm(self, in_d, KC, w_ap, src, dst):
        P = self.P

        def extra(st):
            return {"xt": [self.sb(st, "rxt", [128, 8, 512], F32) for _ in range(2)],
                    "xo": [self.sb(st, "rxo", [128, 8, 512], F32) for _ in range(2)]}

        def pre(ctx, tt, ts):
            b = tt % 2
            P.dma(SP, ctx["xt"][b][:], src[:, ts].rearrange("(kc p) t -> p kc t", p=128),
                  reads=[("xs_d", tt)], writes=[("rxt", b)])

        def epi(ctx, tt, ts, gi, tiles, keys):
            b = tt % 2
            P.v(DVE, "tensor_tensor", ctx["xo"][b][:, gi, :], tiles[0][:], ctx["xt"][b][:, gi, :], ALU.add,
                reads=[keys[0], ("rxt", b)], writes=[("rxo", b, gi)])

        def post(ctx, tt, ts):
            b = tt % 2
            o = P.dma(SP, dst[:, ts].rearrange("(kc p) t -> p kc t", p=128), ctx["xo"][b][:],
                      reads=[("rxo", b, gi) for gi in range(8)], writes=[("xs_d", tt)])
            self.final_ops.append(o)

        self.gemm_fm(in_d, KC, w_ap, [[c] for c in range(0, 1024, 128)], epi, pre, post, extra)

    def ffn(self, i, src, dst):
        P = self.P
        w_gu = self.W["ffn_w_gu"][i]

        def extra(st):
            at1 = self.sb(st, "at", [128, HC, 512], BF16)
            return {"sg": [self.sb(st, "sg", [128, 512], F32) for _ in range(3)],
                    "at": [at1, at1], "n": 0}

        def epi(ctx, tt, ts, gi, tiles, keys):
            b = tt % 2
            k = ctx["n"] % 3
            ctx["n"] += 1
            P.act(ctx["sg"][k][:], tiles[0][:], AF.Silu, reads=[keys[0]], writes=[("sg", k)])
            P.v(DVE, "tensor_tensor", ctx["at"][b][:, gi, :], ctx["sg"][k][:], tiles[1][:], ALU.mult,
                reads=[("sg", k), keys[1]], writes=[("at", 0, gi)])
            if gi == 10:
                P.dma(SP, self.actT[0:11 * 128, ts].rearrange("(kc p) t -> p kc t", p=128), ctx["at"][b][:, 0:11, :],
                      reads=[("at", 0, g_) for g_ in range(11)], writes=[("act_d", tt, 0)])

        def post(ctx, tt, ts):
            b = tt % 2
            P.dma(SP, self.actT[11 * 128:HC * 128, ts].rearrange("(kc p) t -> p kc t", p=128), ctx["at"][b][:, 11:HC, :],
                  reads=[("at", 0, gi) for gi in range(11, HC)], writes=[("act_d", tt, 1)])

        groups = [[j * 128, FFN_H + j * 128] for j in range(HC)]
        self.gemm_fm(self.hT, 8, w_gu, groups, epi, None, post, extra, norm=(src, f"ffnn{i}", False))
        self.resid_gemm(self.actT, HC, self.W["ffn_w_down"][i], src, dst)


    def attn_layer(self, i, j, src, dst):
        P = self.P
        w_qkv = self.W["sb_w_qkv"][j]

        def extra(st):
            return {"sq": [self.sb(st, "qsq", [128, 512], BF16) for _ in range(3)],
                    "r": [self.sb(st, "qr", [128, 512], F32) for _ in range(3)],
                    "ps2": [self.ps(st, "qps2") for _ in range(3)],
                    "qo": [self.sb(st, "qo", [128, 16, 512], BF16) for _ in range(2)], "n": 0}

        def epi(ctx, tt, ts, gi, tiles, keys):
            b = tt % 2
            k = ctx["n"] % 3
            ctx["n"] += 1
            isq = gi < 8
            gcol = self.col(f"qg{j}" if isq else f"kg{j}")
            P.act(ctx["sq"][k][:], tiles[0][:], AF.Square, reads=[keys[0]], writes=[("qsq", k)])
            P.mm(ctx["ps2"][k][:], self.blk_bf[:], ctx["sq"][k][:], reads=[("qsq", k)], writes=[("qps2", k)])
            if ctx.get("pend"):
                ctx.pop("pend")()

            def fin(k=k, b=b, gi=gi, isq=isq, gcol=gcol, t0=tiles[0], k0=keys[0]):
                P.act(ctx["r"][k][:], ctx["ps2"][k][:], AF.Ln, bias=EPS, scale=1.0 / 64,
                      reads=[("qps2", k)], writes=[("qr", k)])
                P.act(ctx["r"][k][:], ctx["r"][k][:], AF.Exp, scale=-0.5, bias=(-2.0794415416798357 if isq else 0.0),
                      reads=[("qr", k)], writes=[("qr", k)])
                P.v(DVE, "scalar_tensor_tensor", ctx["qo"][b][:, gi, :], t0[:], gcol, ctx["r"][k][:],
                    ALU.mult, ALU.mult, reads=[k0, ("qr", k)], writes=[("qo", b, gi)])
            ctx["pend"] = fin

        def post(ctx, tt, ts):
            b = tt % 2
            if ctx.get("pend"):
                ctx.pop("pend")()
            P.dma(SP, self.qT[:, ts].rearrange("(kc p) t -> p kc t", p=128), ctx["qo"][b][:, 0:8, :],
                  reads=[("qo", b, gi) for gi in range(8)], writes=[("qT_d", tt)])
            P.dma(SP, self.kT[:, ts].rearrange("(kc p) t -> p kc t", p=128), ctx["qo"][b][:, 8:16, :],
                  reads=[("qo", b, gi) for gi in range(8, 16)], writes=[("kT_d", tt)])

        self.gemm_fm(self.hT, 8, w_qkv, [[c] for c in range(0, 2048, 128)], epi, None, post, extra, npsum=4,
                     norm=(src, f"mixn{i}", True))

        with self.phase() as st:
            wv = self.sb(st, "wv", [128, 8, 1024], BF16)
            self.load_w(wv, "wv", w_qkv, 2048, 3072, 8)
            wk = [("wv", kc) for kc in range(8)]
            it = [self.sb(st, "vit", [128, 8, 512], BF16) for _ in range(2)]
            vo = [self.sb(st, "vo", [128, 4, 1024], BF16) for _ in range(2)]
            pv = [self.ps(st, "pv") for _ in range(4)]
            pi = 0
            for tt in range(NT):
                b = tt % 2
                ts = slice(tt * 512, (tt + 1) * 512)
                P.dma(SP, it[b][:], self.hT[:, ts].rearrange("(kc p) t -> p kc t", p=128), writes=[("vit", b)])
                for bl in range(4):
                    for cs in range(2):
                        pt = pv[pi % 4]; key = ("pv", pi % 4); pi += 1
                        for kc in range(8):
                            P.mm(pt[:], it[b][:, kc, bl * 128:(bl + 1) * 128], wv[:, kc, cs * 512:(cs + 1) * 512],
                                 start=(kc == 0), stop=(kc == 7),
                                 reads=[("vit", b)] + (wk if tt == 0 else []), writes=[key])
                        eng = ACT if (pi % 2) else DVE
                        if eng == ACT:
                            P.act(vo[b][:, bl, cs * 512:(cs + 1) * 512], pt[:], AF.Copy, reads=[key],
                                  writes=[("vo", b, bl, cs)])
                        else:
                            P.v(DVE, "tensor_copy", vo[b][:, bl, cs * 512:(cs + 1) * 512], pt[:], reads=[key],
                                writes=[("vo", b, bl, cs)])
                P.dma(SP, self.vtok[ts, :].rearrange("(bl p) c -> p bl c", p=128), vo[b][:],
                      reads=[("vo", b, bl, cs) for bl in range(4) for cs in range(2)], writes=[("v_d", tt)])

        self.attn_core()
        self.resid_gemm(self.oT, 8, self.W["sb_w_o"][j], src, dst)

    def attn_core(self):
        P = self.P
        with self.phase() as st:
            q2 = [self.sb(st, "q2", [128, T], BF16) for _ in range(2)]
            k2 = [self.sb(st, "k2", [128, T], BF16) for _ in range(2)]
            v2 = [self.sb(st, "v2", [128, 32, 128], BF16) for _ in range(2)]
            NB = 4
            e_ = [self.sb(st, "ae", [128, 512], F32) for _ in range(NB)]
            sp_ = [self.sb(st, "asp", [128, 512], F32) for _ in range(NB)]
            nsp = [self.sb(st, "ansp", [128, 512], BF16) for _ in range(NB)]
            t1 = [self.sb(st, "at1", [128, 512], F32) for _ in range(NB)]
            t2 = [self.sb(st, "at2", [128, 512], F32) for _ in range(NB)]
            w_ = [self.sb(st, "aw", [128, 512], BF16) for _ in range(NB)]
            ob = [self.sb(st, "aob", [64, 512], BF16) for _ in range(2)]
            Z = [self.ps(st, "aZ") for _ in range(2)]
            S = [self.ps(st, "aS") for _ in range(2)]
            O = [self.ps(st, "aO") for _ in range(2)]

            blocks = []
            for jp in range(self.attn_pairs):
                for hh in range(2):
                    for qt in range(NT):
                        nk = 4 * qt + 4
                        for n, kb in enumerate(reversed(range(nk))):
                            blocks.append((jp, hh, qt, kb, n, nk))
            NBLK = len(blocks)

            def load_pair(jp):
                b = jp % 2
                rs_ = slice(jp * 128, (jp + 1) * 128)
                P.dma(SP, q2[b][:], self.qT[rs_, :], writes=[("q2", b)])
                P.dma(SP, k2[b][:], self.kT[rs_, :], writes=[("k2", b)])
                P.dma(SP, v2[b][:], self.vtok[:, rs_].rearrange("(bl p) c -> p bl c", p=128), writes=[("v2", b)])

            load_pair(0)
            qtile_id = {}
            cnt = 0
            for bi, (jp, hh, qt, kb, n, nk) in enumerate(blocks):
                if n == 0:
                    qtile_id[(jp, hh, qt)] = cnt
                    cnt += 1

            def stageA(i):
                jp, hh, qt, kb, n, nk = blocks[i]
                b = jp % 2
                hp = slice(hh * 64, (hh + 1) * 64)
                z = Z[i % 2]; zk = ("aZ", i % 2)
                k = i % NB
                P.mm(z[:], k2[b][hp, kb * 128:(kb + 1) * 128], q2[b][hp, qt * 512:(qt + 1) * 512],
                     reads=[("q2", b), ("k2", b)], writes=[zk])

            def stageA2(i):
                jp, hh, qt, kb, n, nk = blocks[i]
                z = Z[i % 2]; zk = ("aZ", i % 2)
                k = i % NB
                P.act(e_[k][:], z[:], AF.Exp, reads=[zk], writes=[("ae", k)])
                P.act(sp_[k][:], e_[k][:], AF.Ln, bias=1.0, reads=[("ae", k)], writes=[("asp", k)])
                m = kb - 4 * qt
                if m >= 0:
                    P.v(DVE, "tensor_tensor", nsp[k][:], sp_[k][:], self.mneg[:, m, :], ALU.mult,
                        reads=[("asp", k)], writes=[("ansp", k)])
                else:
                    P.v(DVE, "tensor_scalar", nsp[k][:], sp_[k][:], -1.0, None, ALU.mult,
                        reads=[("asp", k)], writes=[("ansp", k)])
                P.v(DVE, "tensor_tensor", t1[k][:], z[:], sp_[k][:], ALU.subtract,
                    reads=[zk, ("asp", k)], writes=[("at1", k)])

            def stageB(i):
                jp, hh, qt, kb, n, nk = blocks[i]
                k = i % NB
                kp = (i - 1) % NB
                X = S[n % 2]; Xk = ("aS", n % 2)
                Oo = S[(n + 1) % 2]; Ok = ("aS", (n + 1) % 2)
                P.mm(X[:], self.tri_bf[:], nsp[k][:], start=(n == 0), stop=True, reads=[("ansp", k)], writes=[Xk])
                P.v(DVE, "tensor_tensor", t2[k][:], X[:], t1[k][:], ALU.add,
                    reads=[Xk, ("at1", k)], writes=[("at2", k)])
                if n < nk - 1:
                    if n == 0:
                        P.mm(Oo[:], self.ones_bf[:], nsp[k][:], start=True, stop=True,
                             reads=[("ansp", k)], writes=[Ok])
                    else:
                        P.mm(Oo[:], self.cmp_bf[:], nsp[kp][:], start=False, stop=True,
                             reads=[("ansp", kp)], writes=[Ok])
                        P.mm(Oo[:], self.ones_bf[:], nsp[k][:], start=False, stop=True,
                             reads=[("ansp", k)], writes=[Ok])

            def stageC(i):
                jp, hh, qt, kb, n, nk = blocks[i]
                k = i % NB
                P.act(w_[k][:], t2[k][:], AF.Exp, reads=[("at2", k)], writes=[("aw", k)])
                m = kb - 4 * qt
                if m >= 0:
                    P.v(POOL, "tensor_tensor", w_[k][:], w_[k][:], self.m01[:, m, :], ALU.mult,
                        reads=[("aw", k)], writes=[("aw", k)])

            def stageD(i):
                jp, hh, qt, kb, n, nk = blocks[i]
                k = i % NB
                b = jp % 2
                qid = qtile_id[(jp, hh, qt)]
                o = O[qid % 2]; ok = ("aO", qid % 2)
                P.mm(o[0:64, :], v2[b][:, kb, hh * 64:(hh + 1) * 64], w_[k][:], start=(n == 0), stop=(n == nk - 1),
                     reads=[("v2", b), ("aw", k)], writes=[ok])
                if n == nk - 1:
                    ob_ = ob[qid % 2]; obk = ("aob", qid % 2)
                    P.v(DVE, "tensor_copy", ob_[:], o[0:64, :], reads=[ok], writes=[obk])
                    r0 = (jp * 2 + hh) * 64
                    P.dma(SP, self.oT[r0:r0 + 64, qt * 512:(qt + 1) * 512], ob_[:], reads=[obk],
                          writes=[("oT_d", jp, hh, qt)])

            P.relax = True
            for i in range(NBLK + 3):
                if i < NBLK:
                    stageA(i)
                if 0 <= i - 1 < NBLK:
                    stageB(i - 1)
                if i < NBLK:
                    stageA2(i)
                if 0 <= i - 2 < NBLK:
                    stageC(i - 2)
                if 0 <= i - 3 < NBLK:
                    stageD(i - 3)
                    jp_, hh_, qt_, kb_, n_, nk_ = blocks[i - 3]
                    if hh_ == 0 and qt_ == 0 and n_ == 0 and jp_ + 1 < self.attn_pairs:
                        load_pair(jp_ + 1)
            P.relax = False

    def gmlp_layer(self, i, src, dst):
        P = self.P
        w_in = self.W["gm_w_in"][0]

        def extra(st):
            return {"uo": [self.sb(st, "uo", [128, 16, 512], BF16) for _ in range(2)]}

        def epi(ctx, tt, ts, gi, tiles, keys):
            b = tt % 2
            P.act(ctx["uo"][b][:, gi, :], tiles[0][:], AF.Gelu, bias=self.col("gm_bu", gi),
                  reads=[keys[0]], writes=[("uo", b, gi)])

        def post(ctx, tt, ts):
            b = tt % 2
            P.dma(SP, self.uT[:, ts].rearrange("(kc p) t -> p kc t", p=128), ctx["uo"][b][:],
                  reads=[("uo", b, gi) for gi in range(16)], writes=[("uT_d", tt)])

        self.gemm_fm(self.hT, 8, w_in, [[c] for c in range(0, 2048, 128)], epi, None, post, extra,
                     norm=(src, f"mixn{i}", True))

        with self.phase() as st:
            wv = self.sb(st, "gwv", [128, 8, 2048], BF16)
            self.load_w(wv, "gwv", w_in, 2048, 4096, 8)
            wk = [("gwv", kc) for kc in range(8)]
            wsf = self.sb(st, "wsf", [128, 16, 128], F32)
            wsb_ = self.sb(st, "wsb_", [128, 16, 128], BF16)
            P.dma(SP, wsf[:], self.wsT_in, writes=["wsf"])
            for g in range(16):
                P.v(POOL, "tensor_tensor", wsb_[:, g, :], wsf[:, g, :], self.maskLS[:], ALU.mult,
                    reads=["wsf"], writes=[("wsb_", g)])
            wsk = [("wsb_", g) for g in range(16)]
            bv = self.sb(st, "gbv", [128, 2048], F32)
            bs = self.sb(st, "gbs", [128, 16, 128], F32)
            P.dma(SP, bv[:], self.row_bc("gm_bv"), writes=["gbv"])
            P.dma(SP, bs[:].rearrange("p g t -> p (g t)"), self.row_bc("gm_bs"), writes=["gbs"])
            it = [self.sb(st, "git", [128, 8, 512], BF16) for _ in range(2)]
            ut = [self.sb(st, "gut", [128, 16, 512], BF16) for _ in range(2)]
            po = [self.sb(st, "gpo", [128, 16, 512], BF16) for _ in range(2)]
            gv = [self.sb(st, "ggv", [128, 2048], F32) for _ in range(2)]
            vs = [self.sb(st, "gvs", [128, 2048], BF16) for _ in range(2)]
            junk = self.sb(st, "gjunk", [128, 512], F32)
            ss = [self.sb(st, "gss", [128, 4], F32) for _ in range(2)]
            rr = [self.sb(st, "grr", [128, 1], F32) for _ in range(2)]
            tm = [self.sb(st, "gtm", [128, 4, 128], F32) for _ in range(2)]
            pv = [self.ps(st, "gpv") for _ in range(4)]
            pm = [self.ps(st, "gpm") for _ in range(2)]
            pi = 0
            mi = 0
            blk = 0
            for tt in range(NT):
                b = tt % 2
                ts = slice(tt * 512, (tt + 1) * 512)
                P.dma(SP, it[b][:], self.hT[:, ts].rearrange("(kc p) t -> p kc t", p=128), writes=[("git", b)])
                P.dma(SP, ut[b][:], self.uT[:, ts].rearrange("(kc p) t -> p kc t", p=128), writes=[("gut", b)])
                for bl in range(4):
                    c2 = blk % 2
                    blk += 1
                    bsl = slice(bl * 128, (bl + 1) * 128)
                    for cs in range(4):
                        pt = pv[pi % 4]; key = ("gpv", pi % 4); pi += 1
                        csl = slice(cs * 512, (cs + 1) * 512)
                        for kc in range(8):
                            P.mm(pt[:], it[b][:, kc, bsl], wv[:, kc, csl], start=(kc == 0), stop=(kc == 7),
                                 reads=[("git", b)] + (wk if tt == 0 else []), writes=[key])
                        P.v(DVE, "tensor_tensor", gv[c2][:, csl], pt[:], bv[:, csl], ALU.add,
                            reads=[key, "gbv"], writes=[("ggv", c2, cs)])
                        P.act(gv[c2][:, csl], gv[c2][:, csl], AF.Gelu, reads=[("ggv", c2, cs)], writes=[("ggv", c2, cs)])
                        P.act(junk[:], gv[c2][:, csl], AF.Square, accum_out=ss[c2][:, cs:cs + 1],
                              reads=[("ggv", c2, cs)], writes=["gjunk", ("gss", c2, cs)])
                    P.v(DVE, "tensor_reduce", rr[c2][:], ss[c2][:], mybir.AxisListType.X, ALU.add,
                        reads=[("gss", c2, cs) for cs in range(4)], writes=[("grr", c2)])
                    P.act(rr[c2][:], rr[c2][:], AF.Sqrt, bias=EPS, scale=1.0 / 2048, reads=[("grr", c2)], writes=[("grr", c2)])
                    P.v(DVE, "reciprocal", rr[c2][:], rr[c2][:], reads=[("grr", c2)], writes=[("grr", c2)])
                    P.v(DVE, "tensor_scalar", vs[c2][:], gv[c2][:], rr[c2][:, 0:1], None, ALU.mult,
                        reads=[("ggv", c2, cs) for cs in range(4)] + [("grr", c2)], writes=[("gvs", c2)])
                    for g4 in range(4):
                        pmt = pm[mi % 2]; pmk = ("gpm", mi % 2)
                        tmt = tm[mi % 2]; tmk = ("gtm", mi % 2)
                        mi += 1
                        for gg in range(4):
                            g = g4 * 4 + gg
                            P.mm(pmt[:, gg * 128:(gg + 1) * 128], vs[c2][:, g * 128:(g + 1) * 128], wsb_[:, g, :],
                                 reads=[("gvs", c2)] + (wsk if blk == 1 else []), writes=[pmk])
                        gsl = slice(g4 * 4, g4 * 4 + 4)
                        o_, _ = COL["gm_vg"]
                        gain_bc = self.cols[:, o_ + g4 * 4:o_ + g4 * 4 + 4].unsqueeze(2).broadcast_to([128, 4, 128])
                        P.v(DVE, "tensor_tensor", tmt[:], pmt[:].rearrange("p (g t) -> p g t", g=4), gain_bc, ALU.mult,
                            reads=[pmk], writes=[tmk])
                        P.v(DVE, "tensor_tensor", tmt[:], tmt[:], bs[:, gsl, :], ALU.add,
                            reads=[tmk, "gbs"], writes=[tmk])
                        P.v(DVE, "tensor_tensor", po[b][:, gsl, bsl], tmt[:], ut[b][:, gsl, bsl], ALU.mult,
                            reads=[tmk, ("gut", b)], writes=[("gpo", b, bl, g4)])
                P.dma(SP, self.prodT[:, ts].rearrange("(kc p) t -> p kc t", p=128), po[b][:],
                      reads=[("gpo", b, bl, g4) for bl in range(4) for g4 in range(4)], writes=[("prod_d", tt)])

        self.resid_gemm(self.prodT, 16, self.W["gm_w_out"][0], src, dst)

    def ssd_layer(self, i, src, dst):
        P = self.P
        w_in = self.W["ssm_w_in"][0]

        for cp in range(4):
            def extra(st):
                return {"raw": [self.sb(st, "craw", [128, 515], F32) for _ in range(8)],
                        "acc": [self.sb(st, "cacc", [128, 512], F32) for _ in range(3)],
                        "xo": [self.sb(st, "cxo", [128, 8, 512], BF16) for _ in range(2)], "n": 0}

            def epi(ctx, tt, ts, gi, tiles, keys, cp=cp):
                b = tt % 2
                ch = cp * 8 + gi
                raw = ctx["raw"][gi]; rk = ("craw", gi)
                k = ctx["n"] % 3
                ctx["n"] += 1
                acc = ctx["acc"][k]; ak = ("cacc", k)
                if tt == 0:
                    P.v(POOL, "memset", raw[:, 0:3], 0.0, writes=[rk])
                P.act(raw[:, 3:515], tiles[0][:], AF.Copy, reads=[keys[0], rk], writes=[rk])
                o_, _ = COL["conv_w"]
                wc = lambda kk: self.cols[:, o_ + ch * 4 + kk:o_ + ch * 4 + kk + 1]
                P.act(acc[:], raw[:, 0:512], AF.Identity, bias=self.col("conv_b", ch), scale=wc(0),
                      reads=[rk], writes=[ak])
                P.relax = True
                for kk in range(1, 4):
                    P.v(DVE, "scalar_tensor_tensor", acc[:], raw[:, kk:kk + 512], wc(kk), acc[:], ALU.mult, ALU.add,
                        reads=[rk, ak], writes=[ak])
                P.relax = False
                if ctx.get("pend"):
                    ctx.pop("pend")()

                def fin(acc=acc, ak=ak, raw=raw, rk=rk, b=b, gi=gi):
                    P.act(ctx["xo"][b][:, gi, :], acc[:], AF.Silu, reads=[ak], writes=[("cxo", b, gi)])
                    P.v(POOL, "tensor_copy", raw[:, 0:3], raw[:, 512:515], reads=[rk, ak], writes=[rk])
                ctx["pend"] = fin

            def post(ctx, tt, ts, cp=cp):
                b = tt % 2
                if ctx.get("pend"):
                    ctx.pop("pend")()
                P.dma(SP, self.xbcT[cp * 1024:(cp + 1) * 1024, ts].rearrange("(kc p) t -> p kc t", p=128),
                      ctx["xo"][b][:], reads=[("cxo", b, gi) for gi in range(8)], writes=[("xbc_d", cp, tt)])

            self.gemm_fm(self.hT, 8, w_in, [[2048 + cp * 1024 + c] for c in range(0, 1024, 128)],
                         epi, None, post, extra, norm=((src, f"mixn{i}", True) if cp == 0 else None))

        self.ssd_core(w_in)
        self.resid_gemm(self.ynT, 16, self.W["ssm_w_out"][0], src, dst)

    def ssd_core(self, w_in):
        P = self.P
        AX = mybir.AxisListType.X
        with self.phase() as st:
            wz = self.sb(st, "wz", [128, 8, 2048], BF16)
            self.load_w(wz, "wz", w_in, 0, 2048, 8)
            wdt = self.sb(st, "wdt", [128, 8, 32], BF16)
            self.load_w(wdt, "wdt", w_in, 6144, 6176, 8)
            wzk = [("wz", kc) for kc in range(8)]
            wdk = [("wdt", kc) for kc in range(8)]
            dtb = self.sb(st, "dtb", [128, 32], F32)
            A_ = self.sb(st, "A_", [128, 32], F32)
            Dk = self.sb(st, "Dk", [128, 32], F32)
            ng = self.sb(st, "ng", [128, 2048], F32)
            P.dma(SP, dtb[:], self.row_bc("dt_bias"), writes=["dtb"])
            P.dma(SP, A_[:], self.row_bc("a_log"), writes=["A_"])
            P.dma(SP, Dk[:], self.row_bc("d_skip"), writes=["Dk"])
            P.dma(SP, ng[:], self.row_bc("ssm_ng"), writes=["ng"])
            P.act(A_[:], A_[:], AF.Exp, reads=["A_"], writes=["A_"])
            P.v(DVE, "tensor_scalar", A_[:], A_[:], -1.0, None, ALU.mult, reads=["A_"], writes=["A_"])
            prev = self.sb(st, "prev", [128, 8, 256], F32)
            prevb = self.sb(st, "prevb", [128, 8, 256], BF16)
            P.v(POOL, "memset", prev[:], 0.0, writes=[("prev", g) for g in range(8)])
            P.v(POOL, "memset", prevb[:], 0.0, writes=[("prevb", g) for g in range(8)])
            it = [self.sb(st, "sit", [128, 8, 512], BF16) for _ in range(2)]
            xb = [self.sb(st, "sxb", [128, 32, 128], BF16) for _ in range(2)]
            ynT_t = [self.sb(st, "synT", [128, 16, 512], BF16) for _ in range(2)]
            sm = lambda n: self.sb(st, n, [128, 32], F32)
            dtr, dmx, dmn, dte, dt_, a_, acum, dd, dtdec, eac, cd = [sm(f"s{n}") for n in
                ("dtr", "dmx", "dmn", "dte", "dt", "a", "acum", "dd", "dtdec", "eac", "cd")]
            nacum = sm("snacum")
            ahi = self.sb(st, "sahi", [128, 32], BF16)
            alo = self.sb(st, "salo", [128, 32], BF16)
            sz = self.sb(st, "ssz", [128, 2048], BF16)
            Xt = self.sb(st, "sXt", [128, 2048], BF16)
            Bt = self.sb(st, "sBt", [128, 8, 128], BF16)
            xdt = self.sb(st, "sxdt", [128, 2048], BF16)
            xdd = self.sb(st, "sxdd", [128, 2048], BF16)
            ytok = self.sb(st, "sytok", [128, 2048], F32)
            tmpx = self.sb(st, "stmpx", [128, 2048], F32)
            yn = self.sb(st, "syn", [128, 2048], BF16)
            cbm = [self.sb(st, "scbm", [128, 128], F32) for _ in range(3)]
            seg = [self.sb(st, "sseg", [128, 4, 128], F32) for _ in range(3)]
            Mm = [self.sb(st, "sM", [128, 4, 128], BF16) for _ in range(3)]
            nmask = self.sb(st, "snmask", [128, 4, 128], BF16)
            P.v(POOL, "memset", nmask[:], 0.0, writes=["snmask"])
            P.op(POOL, lambda e: e.affine_select(nmask[:], nmask[:], [[0, 4], [1, 128]], ALU.is_ge, -30000.0,
                                                 base=0, channel_multiplier=-1), reads=["snmask"], writes=["snmask"])
            yo = [self.sb(st, "syo", [128, 256], F32) for _ in range(2)]
            ptmp = [self.sb(st, "sptmp", [128, 256], F32) for _ in range(2)]
            junk = self.sb(st, "sjunk", [128, 256], F32)
            ssq = self.sb(st, "sssq", [128, 8], F32)
            rstd = self.sb(st, "srstd", [128, 8], F32)
            pzt = [self.ps(st, "spz") for _ in range(2)]
            pab = [self.ps(st, "spab") for _ in range(2)]
            pys = [self.ps(st, "spy") for _ in range(2)]
            pcss = [self.ps(st, "spcs") for _ in range(2)]
            psm = pcss[1]

            hv = lambda t: t[:].rearrange("p (h d) -> p h d", d=64)
            bc64 = lambda ap_: ap_.unsqueeze(2).broadcast_to([128, ap_.shape[1], 64])
            zi = 0
            ti = 0
            hi = 0
            for tt in range(NT):
                b = tt % 2
                ts = slice(tt * 512, (tt + 1) * 512)
                P.dma(SP, it[b][:], self.hT[:, ts].rearrange("(kc p) t -> p kc t", p=128), writes=[("sit", b)])
                for bl in range(4):
                    c = tt * 4 + bl
                    xbb = xb[c % 2]; xk = ("sxb", c % 2)
                    csl = slice(c * 128, (c + 1) * 128)
                    bsl = slice(bl * 128, (bl + 1) * 128)
                    P.dma(SP, xbb[:], self.xbcT[:, csl].rearrange("(kc p) t -> p kc t", p=128), writes=[xk])
                    first = (c == 0)
                    for kc in range(8):
                        P.mm(psm[:, 0:32], it[b][:, kc, bsl], wdt[:, kc, :], start=(kc == 0), stop=(kc == 7),
                             reads=[("sit", b)] + (wdk if first else []), writes=[("spcs", 1)])
                    P.v(DVE, "tensor_tensor", dtr[:], psm[:, 0:32], dtb[:], ALU.add, reads=[("spcs", 1), "dtb"], writes=["dtr"])
                    P.v(DVE, "tensor_scalar", dmx[:], dtr[:], 0.0, None, ALU.max, reads=["dtr"], writes=["dmx"])
                    P.v(DVE, "tensor_scalar", dmn[:], dtr[:], 0.0, None, ALU.min, reads=["dtr"], writes=["dmn"])
                    P.v(DVE, "tensor_tensor", dmn[:], dmn[:], dmx[:], ALU.subtract, reads=["dmn", "dmx"], writes=["dmn"])
                    P.act(dte[:], dmn[:], AF.Exp, reads=["dmn"], writes=["dte"])
                    P.act(dte[:], dte[:], AF.Ln, bias=1.0, reads=["dte"], writes=["dte"])
                    P.v(DVE, "tensor_tensor", dt_[:], dmx[:], dte[:], ALU.add, reads=["dmx", "dte"], writes=["dt"])
                    P.v(DVE, "tensor_tensor", a_[:], dt_[:], A_[:], ALU.mult, reads=["dt", "A_"], writes=["a"])
                    P.v(DVE, "tensor_copy", ahi[:], a_[:], reads=["a"], writes=["ahl"])
                    P.v(DVE, "tensor_tensor", alo[:], a_[:], ahi[:], ALU.subtract, reads=["a", "ahl"], writes=["ahl"])
                    P.mm(psm[:, 32:64], self.triU_f[:], a_[:], reads=["a"], writes=[("spcs", 1)])
                    P.mm(psm[:, 64:96], self.ones_f[:], a_[:], reads=["a"], writes=[("spcs", 1)])
                    P.act(acum[:], psm[:, 32:64], AF.Copy, reads=[("spcs", 1)], writes=["acum"])
                    P.act(nacum[:], psm[:, 32:64], AF.Copy, scale=-1.0, reads=[("spcs", 1)], writes=["nacum"])
                    P.v(DVE, "tensor_tensor", dd[:], psm[:, 64:96], acum[:], ALU.subtract,
                        reads=[("spcs", 1), "acum"], writes=["dd"])
                    P.act(dd[:], dd[:], AF.Exp, reads=["dd"], writes=["dd"])
                    P.v(DVE, "tensor_tensor", dtdec[:], dt_[:], dd[:], ALU.mult, reads=["dt", "dd"], writes=["dtdec"])
                    P.act(eac[:], acum[:], AF.Exp, reads=["acum"], writes=["eac"])
                    P.act(cd[:], psm[:, 64:96], AF.Exp, reads=[("spcs", 1)], writes=["cd"])
                    if CUT <= 1:
                        continue
                    for cs in range(4):
                        pt = pzt[zi % 2]; zk = ("spz", zi % 2); zi += 1
                        for kc in range(8):
                            P.mm(pt[:], it[b][:, kc, bsl], wz[:, kc, cs * 512:(cs + 1) * 512],
                                 start=(kc == 0), stop=(kc == 7),
                                 reads=[("sit", b)] + (wzk if first else []), writes=[zk])
                        P.act(sz[:, cs * 512:(cs + 1) * 512], pt[:], AF.Silu, reads=[zk], writes=[("ssz", cs)])
                    if CUT <= 2:
                        continue
                    for q4 in range(6):
                        pt = pzt[zi % 2]; tk = ("spz", zi % 2); zi += 1
                        for u in range(4):
                            ch = q4 * 4 + u
                            P.mm(pt[:, u * 128:(u + 1) * 128], xbb[:, ch, :], self.ident[:],
                                 reads=[xk], writes=[tk])
                        if q4 < 4:
                            P.act(Xt[:, q4 * 512:(q4 + 1) * 512], pt[:], AF.Copy, reads=[tk], writes=[("sXt", q4)])
                        else:
                            P.act(Bt[:, (q4 - 4) * 4:(q4 - 4) * 4 + 4, :].rearrange("p g n -> p (g n)"), pt[:], AF.Copy,
                                  reads=[tk], writes=[("sBt", q4 - 4)])
                    xtk = [("sXt", q4) for q4 in range(4)]
                    if CUT <= 2.5:
                        continue
                    P.v(DVE, "tensor_tensor", hv(xdt), hv(Xt), bc64(dt_[:, :]), ALU.mult,
                        reads=xtk + ["dt"], writes=["sxdt"])
                    P.v(DVE, "tensor_tensor", hv(xdd), hv(Xt), bc64(dtdec[:, :]), ALU.mult,
                        reads=xtk + ["dtdec"], writes=["sxdd"])
                    P.v(POOL, "tensor_tensor", hv(tmpx), hv(Xt), bc64(Dk[:, :]), ALU.mult,
                        reads=xtk + ["Dk", "sxdd"], writes=["stmpx"])
                    if CUT <= 3:
                        continue
                    def st1(g):
                        cb_ = cbm[g % 3]; cbk = ("scbm", g % 3)
                        pcs = pcss[g % 2]; pck = ("spcs", g % 2)
                        pa = pab[g % 2]; pak = ("spab", g % 2)
                        sg_ = seg[g % 3]; sgk = ("sseg", g % 3)
                        BTg = xbb[:, 16 + g, :]
                        CTg = xbb[:, 24 + g, :]
                        P.mm(pcs[:, 0:128], BTg, CTg, reads=[xk], writes=[pck])
                        P.v(DVE, "tensor_tensor", cb_[:], pcs[:, 0:128], self.maskLS[:], ALU.mult,
                            reads=[pck], writes=[cbk])
                        for r in range(4):
                            h = 4 * g + r
                            P.mm(pa[:, r * 128:(r + 1) * 128], ahi[:, h:h + 1].broadcast_to([128, 128]), self.triU_bf[:],
                                 start=True, stop=False, reads=["ahl"], writes=[pak])
                            P.mm(pa[:, r * 128:(r + 1) * 128], alo[:, h:h + 1].broadcast_to([128, 128]), self.triU_bf[:],
                                 start=False, stop=True, reads=["ahl"], writes=[pak])
                        P.v(DVE, "scalar_tensor_tensor", sg_[:], pa[:].rearrange("p (r l) -> p r l", r=4), 0.0,
                            acum[:, 4 * g:4 * g + 4].unsqueeze(2).broadcast_to([128, 4, 128]), ALU.add, ALU.min,
                            reads=[pak, "acum"], writes=[sgk])
                        for r in range(4):
                            h = 4 * g + r
                            P.act(sg_[:, r, :], sg_[:, r, :], AF.Exp, bias=nacum[:, h:h + 1],
                                  reads=[sgk, "nacum"], writes=[sgk])

                    def st2(g):
                        cb_ = cbm[g % 3]; cbk = ("scbm", g % 3)
                        pcs = pcss[g % 2]; pck = ("spcs", g % 2)
                        sg_ = seg[g % 3]; sgk = ("sseg", g % 3)
                        M_ = Mm[g % 3]; Mk = ("sM", g % 3)
                        py = pys[g % 2]; pyk = ("spy", g % 2)
                        CTg = xbb[:, 24 + g, :]
                        P.v(DVE, "tensor_tensor", M_[:], sg_[:], cb_[:].unsqueeze(1).broadcast_to([128, 4, 128]), ALU.mult,
                            reads=[sgk, cbk], writes=[Mk])
                        for r in range(4):
                            h = 4 * g + r
                            P.mm(py[:, r * 64:(r + 1) * 64], M_[:, r, :], xdt[:, h * 64:(h + 1) * 64],
                                 reads=[Mk, "sxdt"], writes=[pyk])
                        P.mm(py[:, 256:512], CTg, prevb[:, g, :], reads=[xk, ("prevb", g)], writes=[pyk])
                        P.mm(pcs[:, 128:384], Bt[:, g, :], xdd[:, g * 256:(g + 1) * 256],
                             reads=[("sBt", g // 4), "sxdd"], writes=[pck])
                        pt_ = ptmp[g % 2]; ptk = ("sptmp", g % 2)
                        P.v(POOL, "tensor_tensor", pt_[:].rearrange("p (h d) -> p h d", d=64),
                            prev[:, g, :].rearrange("p (h d) -> p h d", d=64), bc64(cd[:, 4 * g:4 * g + 4]), ALU.mult,
                            reads=[("prev", g), "cd"], writes=[ptk])

                    def st3(g):
                        pcs = pcss[g % 2]; pck = ("spcs", g % 2)
                        py = pys[g % 2]; pyk = ("spy", g % 2)
                        yo_ = yo[g % 2]; yok = ("syo", g % 2)
                        pt_ = ptmp[g % 2]; ptk = ("sptmp", g % 2)
                        P.v(DVE, "tensor_tensor", yo_[:].rearrange("p (h d) -> p h d", d=64),
                            py[:, 256:512].rearrange("p (h d) -> p h d", d=64), bc64(eac[:, 4 * g:4 * g + 4]), ALU.mult,
                            reads=[pyk, "eac"], writes=[yok])
                        P.v(DVE, "tensor_tensor", ytok[:, g * 256:(g + 1) * 256], py[:, 0:256], yo_[:], ALU.add,
                            reads=[pyk, yok], writes=[("sytok", g)])
                        P.v(DVE, "tensor_tensor", prev[:, g, :], pt_[:], pcs[:, 128:384], ALU.add,
                            reads=[ptk, pck], writes=[("prev", g)])
                        P.v(POOL, "tensor_copy", prevb[:, g, :], prev[:, g, :], reads=[("prev", g)], writes=[("prevb", g)])

                    for step in range(8 + 2):
                        if step < 8:
                            st1(step)
                        if 0 <= step - 1 < 8:
                            st2(step - 1)
                        if 0 <= step - 2 < 8:
                            st3(step - 2)
                    if CUT <= 4:
                        continue
                    yk = [("sytok", g) for g in range(8)]
                    P.v(DVE, "tensor_tensor", ytok[:], ytok[:], tmpx[:], ALU.add, reads=yk + ["stmpx"], writes=yk)
                    P.v(DVE, "tensor_tensor", ytok[:], ytok[:], sz[:], ALU.mult,
                        reads=yk + [("ssz", cs) for cs in range(4)], writes=yk)
                    for g in range(8):
                        P.act(junk[:], ytok[:, g * 256:(g + 1) * 256], AF.Square, accum_out=ssq[:, g:g + 1],
                              reads=[("sytok", g)], writes=["sjunk", ("sssq", g)])
                    P.act(rstd[:], ssq[:], AF.Sqrt, bias=EPS, scale=1.0 / 256,
                          reads=[("sssq", g) for g in range(8)], writes=["srstd"])
                    P.v(DVE, "reciprocal", rstd[:], rstd[:], reads=["srstd"], writes=["srstd"])
                    P.v(DVE, "tensor_tensor", ytok[:].rearrange("p (g d) -> p g d", d=256),
                        ytok[:].rearrange("p (g d) -> p g d", d=256),
                        rstd[:, :].unsqueeze(2).broadcast_to([128, 8, 256]), ALU.mult,
                        reads=yk + ["srstd"], writes=yk)
                    P.v(DVE, "tensor_tensor", yn[:], ytok[:], ng[:], ALU.mult, reads=yk + ["ng"], writes=["syn"])
                    for q4 in range(4):
                        pt = pzt[zi % 2]; tk = ("spz", zi % 2); zi += 1
                        for u in range(4):
                            ch = q4 * 4 + u
                            P.mm(pt[:, u * 128:(u + 1) * 128], yn[:, ch * 128:(ch + 1) * 128], self.ident[:],
                                 reads=["syn"], writes=[tk])
                        P.act(ynT_t[b][:, q4 * 4:q4 * 4 + 4, bsl], pt[:].rearrange("p (u t) -> p u t", u=4), AF.Copy,
                              reads=[tk], writes=[("synT", b, bl, q4)])
                if CUT <= 4:
                    continue
                P.dma(SP, self.ynT[:, ts].rearrange("(kc p) t -> p kc t", p=128), ynT_t[b][:],
                      reads=[("synT", b, bl, q4) for bl in range(4) for q4 in range(4)], writes=[("ynT_d", tt)])

    def build(self):
        self.declare_io()
        self.final_ops = []
        with contextlib.ExitStack() as st:
            self.setup_consts(st)
            if self.plan is not None:
                for pi_, (what, i) in enumerate(self.plan):
                    src = self.xT_in if pi_ == 0 else self.xs
                    dst = self.yT_out if pi_ == len(self.plan) - 1 else self.xs
                    self.final_ops = []
                    if what == "ffn":
                        self.ffn(i, src, dst)
                    elif what == "attn":
                        self.attn_layer(i, i // 3, src, dst)
                    elif what == "gmlp":
                        self.gmlp_layer(i, src, dst)
                    elif what == "ssd":
                        self.ssd_layer(i, src, dst)
                self.P.emit(final_wait_ops=self.final_ops)
                return self.nc
            nl = len(self.layers)
            for li, i in enumerate(self.layers):
                first, last = (li == 0), (li == nl - 1)
                src = self.xT_in if first else self.xs
                kind = i % 3
                j = i // 3
                mid = self.xs
                if kind == 0:
                    self.attn_layer(i, j, src, mid)
                elif kind == 1:
                    self.gmlp_layer(i, src, mid)
                else:
                    self.ssd_layer(i, src, mid)
                self.final_ops = []
                self.ffn(i, mid, self.yT_out if last else self.xs)
            self.P.emit(final_wait_ops=self.final_ops)
        return self.nc


_CACHE = {}


def kernel(**inputs):
    x = np.asarray(inputs["x"], np.float32)
    cols, rows, wsT = pack_small(inputs)
    if "nc" not in _CACHE:
        plan = []
        for i in range(4):
            plan.append((("attn", "gmlp", "ssd")[i % 3], i))
            plan.append(("ffn", i))
        _CACHE["nc"] = Builder(plan=plan).build()
    nc = _CACHE["nc"]
    shared = {"cols": cols, "rows": rows, "wsT": wsT}
    for k in ("sb_w_qkv", "sb_w_o", "gm_w_in", "gm_w_out", "ssm_w_in", "ssm_w_out", "ffn_w_gu", "ffn_w_down"):
        shared[k] = np.ascontiguousarray(np.asarray(inputs[k], np.float32))
    in_maps = []
    for b in range(8):
        m = dict(shared)
        m["xT"] = np.ascontiguousarray(x[b].T)
        in_maps.append(m)
    res = run_bass_kernel_spmd(nc, in_maps, core_ids=list(range(8)))
    out = np.stack([np.ascontiguousarray(r["yT"].T) for r in res.results], axis=0)
    return out.astype(np.float32)
```

```python
import contextlib
import numpy as np
import concourse.bass as bass
import concourse.mybir as mybir
from concourse.bass_utils import run_bass_kernel_spmd

F32 = mybir.dt.float32
BF16 = mybir.dt.bfloat16
AF = mybir.ActivationFunctionType
ALU = mybir.AluOpType

PE, ACT, DVE, POOL, SP = "tensor", "scalar", "vector", "gpsimd", "sync"
ENGINES = [PE, ACT, DVE, POOL, SP]
SEM_EPOCH = 30000
N_DMA_SEMS = 6
SAME_ENGINE_SYNC = True

import os
CUT = 9
T = 4096
D = 1024
NT = T // 512
EPS = 1e-6
FFN_H = 2816
HC = FFN_H // 128


class Op:
    __slots__ = ("eng", "fn", "deps", "signal", "event", "vc", "is_dma", "waits", "is_mm")

    def __init__(self, eng, fn, is_dma=False, is_mm=False):
        self.eng = eng
        self.fn = fn
        self.deps = []
        self.signal = False
        self.event = None
        self.vc = None
        self.is_dma = is_dma
        self.is_mm = is_mm
        self.waits = []


class Prog:
    def __init__(self, nc):
        self.nc = nc
        self.ops = []
        self.last_w = {}
        self.readers = {}
        self.dma_count = {SP: 0, POOL: 0, ACT: 0}
        self.dma_prev = {}
        self.last_op = {}
        self.phase_dma = []
        self.bar = {}

    relax = False

    def op(self, eng, fn, reads=(), writes=(), is_dma=False, is_mm=False):
        o = Op(eng, fn, is_dma=is_dma, is_mm=is_mm)
        deps = []
        for r in reads:
            w = self.last_w.get(r)
            if w is not None:
                deps.append(w)
        for w_ in writes:
            w = self.last_w.get(w_)
            if w is not None:
                deps.append(w)
            rs = self.readers.get(w_)
            if rs:
                deps.extend(rs)
        if is_dma:
            k = self.dma_count[eng]
            self.dma_count[eng] = k + 1
            slot = (eng, k % N_DMA_SEMS)
            p = self.dma_prev.get(slot)
            if p is not None:
                deps.append(p)
            self.dma_prev[slot] = o
            o.event = slot
            self.phase_dma.append(o)
        b = self.bar.pop(eng, None)
        if b:
            deps.extend(b)
        seen = set()
        for d in deps:
            if d is o or id(d) in seen:
                continue
            seen.add(id(d))
            if (not is_dma) and (not d.is_dma) and d.eng == eng:
                if (o.is_mm and d.is_mm) or (self.relax and eng != POOL) or not SAME_ENGINE_SYNC:
                    continue
            o.deps.append(d)
        for r in reads:
            self.readers.setdefault(r, []).append(o)
        for w_ in writes:
            self.last_w[w_] = o
            self.readers[w_] = []
        self.ops.append(o)
        if not is_dma:
            self.last_op[eng] = o
        return o

    def barrier(self):
        deps = list(self.last_op.values()) + list(self.phase_dma)
        for b in self.bar.values():
            deps.extend(b)
        seen = set()
        dd = []
        for d in deps:
            if id(d) not in seen:
                seen.add(id(d))
                dd.append(d)
        deps = dd
        self.phase_dma = []
        self.bar = {e: list(deps) for e in ENGINES}
        self.last_w = {}
        self.readers = {}

    def mm(self, out, lhsT, rhs, start=True, stop=True, reads=(), writes=()):
        return self.op(PE, lambda e: e.matmul(out, lhsT, rhs, start=start, stop=stop),
                       reads, writes, is_mm=True)

    def transpose(self, out, in_, ident, reads=(), writes=()):
        return self.op(PE, lambda e: e.transpose(out, in_, ident), reads, writes, is_mm=True)

    def act(self, out, in_, func, reads=(), writes=(), **kw):
        return self.op(ACT, lambda e: e.activation(out, in_, func, **kw), reads, writes)

    def dma(self, eng, out, in_, reads=(), writes=(), **kw):
        return self.op(eng, lambda e: e.dma_start(out, in_, **kw), reads, writes, is_dma=True)

    def v(self, eng, name, *args, reads=(), writes=(), **kw):
        return self.op(eng, lambda e: getattr(e, name)(*args, **kw), reads, writes)

    def finalize(self):
        nc = self.nc
        for o in self.ops:
            for d in o.deps:
                d.signal = True
        cnt = {e: 0 for e in ENGINES}
        self.eng_sems = {e: [] for e in ENGINES}
        self.dma_sems = {}
        dma_val = {}
        for o in self.ops:
            if o.is_dma:
                slot = o.event
                if slot not in self.dma_sems:
                    self.dma_sems[slot] = nc.alloc_semaphore(name=f"dma_{slot[0]}_{slot[1]}")
                    dma_val[slot] = 0
                dma_val[slot] += 16
                o.event = (self.dma_sems[slot], dma_val[slot])
                o.signal = True
            elif o.signal:
                c = cnt[o.eng]
                ep = c // SEM_EPOCH
                if ep >= len(self.eng_sems[o.eng]):
                    self.eng_sems[o.eng].append(nc.alloc_semaphore(name=f"s_{o.eng}_{ep}"))
                cnt[o.eng] = c + 1
                o.event = (self.eng_sems[o.eng][ep], c - ep * SEM_EPOCH + 1)
        known = {e: {} for e in ENGINES}
        nwaits = 0
        for o in self.ops:
            kn = known[o.eng]
            w = {}
            for d in o.deps:
                s, v = d.event
                key = id(s)
                if kn.get(key, 0) >= v:
                    continue
                if key not in w or w[key][1] < v:
                    w[key] = (s, v)
            for d in o.deps:
                s, v = d.event
                if id(s) in w and d.vc is not None:
                    for k2, v2 in d.vc.items():
                        if kn.get(k2, 0) < v2:
                            kn[k2] = v2
            for key, (s, v) in w.items():
                if kn.get(key, 0) < v:
                    kn[key] = v
            o.waits = list(w.values())
            nwaits += len(o.waits)
            if o.signal:
                o.vc = dict(kn)
        self.nwaits = nwaits

    def emit(self, final_wait_ops=()):
        self.finalize()
        nc = self.nc
        by_eng = {e: [o for o in self.ops if o.eng == e] for e in ENGINES}
        finals = [o.event for o in final_wait_ops]

        def run(eng_name):
            def body(e):
                for o in by_eng[eng_name]:
                    for (s, v) in o.waits:
                        e.wait_ge(s, v)
                    inst = o.fn(e)
                    if o.signal:
                        s, v = o.event
                        inst.then_inc(s, 16 if o.is_dma else 1)
                if eng_name == SP:
                    mx = {}
                    for (s, v) in finals:
                        if id(s) not in mx or mx[id(s)][1] < v:
                            mx[id(s)] = (s, v)
                    for (s, v) in mx.values():
                        e.wait_ge(s, v)
            return body

        with nc.Block() as block:
            block.tensor(run(PE))
            block.scalar(run(ACT))
            block.vector(run(DVE))
            block.gpsimd(run(POOL))
            block.sync(run(SP))


def _colvec(v):
    return np.ascontiguousarray(np.asarray(v, np.float32).reshape(-1, 128).T)


COL = {}
ROW = {}


def _layout_tables():
    c = 0
    for i in range(4):
        COL[f"mixn{i}"] = (c, 8); c += 8
        COL[f"ffnn{i}"] = (c, 8); c += 8
    for j in range(2):
        COL[f"qg{j}"] = (c, 1); c += 1
        COL[f"kg{j}"] = (c, 1); c += 1
    COL["gm_bu"] = (c, 16); c += 16
    COL["gm_vg"] = (c, 16); c += 16
    COL["conv_w"] = (c, 128); c += 128
    COL["conv_b"] = (c, 32); c += 32
    COL["_n"] = c
    r = 0
    ROW["gm_bv"] = (r, 2048); r += 2048
    ROW["gm_bs"] = (r, 2048); r += 2048
    ROW["ssm_ng"] = (r, 2048); r += 2048
    ROW["dt_bias"] = (r, 32); r += 32
    ROW["a_log"] = (r, 32); r += 32
    ROW["d_skip"] = (r, 32); r += 32
    ROW["_n"] = r


_layout_tables()


def pack_small(inp):
    cols = np.zeros((128, COL["_n"]), np.float32)

    def put(name, arr):
        o, n = COL[name]
        cols[:, o:o + n] = arr

    for i in range(4):
        put(f"mixn{i}", _colvec(inp["mix_norm"][i]))
        put(f"ffnn{i}", _colvec(inp["ffn_norm"][i]))
    for j in range(2):
        put(f"qg{j}", np.tile(np.asarray(inp["sb_q_gain"][j], np.float32), 2)[:, None])
        put(f"kg{j}", np.tile(np.asarray(inp["sb_k_gain"][j], np.float32), 2)[:, None])
    put("gm_bu", _colvec(inp["gm_b_in"][0][:2048]))
    put("gm_vg", _colvec(inp["gm_v_gain"][0]))
    cw = np.asarray(inp["ssm_conv_w"][0], np.float32)
    put("conv_w", np.ascontiguousarray(cw.T.reshape(32, 128, 4).transpose(1, 0, 2)).reshape(128, 128))
    put("conv_b", _colvec(inp["ssm_conv_b"][0]))
    rows = np.zeros((1, ROW["_n"]), np.float32)

    def putr(name, arr):
        o, n = ROW[name]
        rows[0, o:o + n] = np.asarray(arr, np.float32).reshape(-1)

    putr("gm_bv", inp["gm_b_in"][0][2048:])
    putr("gm_bs", inp["gm_b_s"][0])
    putr("ssm_ng", inp["ssm_norm_gain"][0])
    putr("dt_bias", inp["ssm_dt_bias"][0])
    putr("a_log", inp["ssm_a_log"][0])
    putr("d_skip", inp["ssm_d"][0])
    wsT = np.ascontiguousarray(np.asarray(inp["gm_w_s"][0], np.float32).transpose(2, 0, 1))
    return cols, rows, wsT


class Builder:
    def __init__(self, layers=(0, 1, 2, 3), debug=False, plan=None, attn_pairs=8):
        self.attn_pairs = attn_pairs
        self.layers = list(layers)
        self.plan = plan
        self.debug = debug
        self.nc = bass.Bass("TRN2", target_bir_lowering=False)
        self.P = Prog(self.nc)
        self.uid = 0

    def dram_in(self, name, shape, dt=F32):
        return self.nc.dram_tensor(name, list(shape), dt, kind="ExternalInput").ap()

    def dram_scr(self, name, shape, dt):
        kind = "ExternalOutput" if self.debug else "Internal"
        return self.nc.dram_tensor(name, list(shape), dt, kind=kind).ap()

    def sb(self, stack, name, shape, dt):
        self.uid += 1
        return stack.enter_context(self.nc.sbuf_tensor(f"{name}_{self.uid}", list(shape), dt))

    def ps(self, stack, name, shape=(128, 512), dt=F32):
        self.uid += 1
        return stack.enter_context(self.nc.psum_tensor(f"{name}_{self.uid}", list(shape), dt))

    @contextlib.contextmanager
    def phase(self):
        with contextlib.ExitStack() as st:
            yield st
            self.P.barrier()

    def load_w(self, wsb, key, w_ap, c0, c1, KC):
        for kc in range(KC):
            self.P.dma(POOL, wsb[:, kc, :], w_ap[kc * 128:(kc + 1) * 128, c0:c1], writes=[(key, kc)])

    def declare_io(self):
        self.xT_in = self.dram_in("xT", [D, T])
        self.yT_out = self.nc.dram_tensor("yT", [D, T], F32, kind="ExternalOutput").ap()
        self.cols_in = self.dram_in("cols", [128, COL["_n"]])
        self.rows_in = self.dram_in("rows", [1, ROW["_n"]])
        self.wsT_in = self.dram_in("wsT", [128, 16, 128])
        W = {}
        W["sb_w_qkv"] = self.dram_in("sb_w_qkv", [2, 1024, 3072])
        W["sb_w_o"] = self.dram_in("sb_w_o", [2, 1024, 1024])
        W["gm_w_in"] = self.dram_in("gm_w_in", [1, 1024, 4096])
        W["gm_w_out"] = self.dram_in("gm_w_out", [1, 2048, 1024])
        W["ssm_w_in"] = self.dram_in("ssm_w_in", [1, 1024, 6176])
        W["ssm_w_out"] = self.dram_in("ssm_w_out", [1, 2048, 1024])
        W["ffn_w_gu"] = self.dram_in("ffn_w_gu", [4, 1024, 5632])
        W["ffn_w_down"] = self.dram_in("ffn_w_down", [4, 2816, 1024])
        self.W = W
        self.xs = self.dram_scr("xs", [D, T], F32)
        self.hT = self.dram_scr("hT", [D, T], BF16)
        self.actT = self.dram_scr("actT", [FFN_H, T], BF16)
        self.qT = self.dram_scr("qT", [D, T], BF16)
        self.kT = self.dram_scr("kT", [D, T], BF16)
        self.vtok = self.dram_scr("vtok", [T, D], BF16)
        self.oT = self.dram_scr("oT", [D, T], BF16)
        self.uT = self.dram_scr("uT", [2048, T], BF16)
        self.prodT = self.dram_scr("prodT", [2048, T], BF16)
        self.xbcT = self.dram_scr("xbcT", [4096, T], BF16)
        self.ynT = self.dram_scr("ynT", [2048, T], BF16)

    def setup_consts(self, st):
        nc, P = self.nc, self.P
        sb = lambda n, s, d: self.sb(st, n, s, d)
        self.cols = sb("cols", [128, COL["_n"]], F32)
        P.dma(SP, self.cols[:], self.cols_in, writes=["cols"])
        self.ident = sb("ident", [128, 128], BF16)
        self.ones_bf = sb("ones_bf", [128, 128], BF16)
        self.ones_f = sb("ones_f", [128, 128], F32)
        self.blk_f = sb("blk_f", [128, 128], F32)
        self.blk_bf = sb("blk_bf", [128, 128], BF16)
        self.triU_f = sb("triU_f", [128, 128], F32)
        self.tri_bf = sb("tri_bf", [128, 128], BF16)
        self.cmp_bf = sb("cmp_bf", [128, 128], BF16)
        self.maskLS = sb("maskLS", [128, 128], F32)
        self.triU_bf = self.cmp_bf
        self.m01 = sb("m01", [128, 4, 512], BF16)
        self.mneg = sb("mneg", [128, 4, 512], BF16)
        g = POOL
        P.v(g, "memset", self.ident[:], 0.0, writes=["c_ident"])
        P.op(g, lambda e: e.affine_select(self.ident[:], self.ident[:], [[-1, 128]], ALU.not_equal, 1.0,
                                          base=0, channel_multiplier=1), writes=["c_ident"], reads=["c_ident"])
        P.v(g, "memset", self.ones_bf[:], 1.0, writes=["c_ones_bf"])
        P.v(g, "memset", self.ones_f[:], 1.0, writes=["c_ones_f"])
        P.v(g, "memset", self.blk_f[:], 0.0, writes=["c_blk"])
        P.v(g, "memset", self.blk_f[0:64, 0:64], 1.0, writes=["c_blk"], reads=["c_blk"])
        P.v(g, "memset", self.blk_f[64:128, 64:128], 1.0, writes=["c_blk"], reads=["c_blk"])
        P.v(g, "memset", self.blk_bf[:], 0.0, writes=["c_blkb"])
        P.v(g, "memset", self.blk_bf[0:64, 0:64], 1.0, writes=["c_blkb"], reads=["c_blkb"])
        P.v(g, "memset", self.blk_bf[64:128, 64:128], 1.0, writes=["c_blkb"], reads=["c_blkb"])
        P.v(g, "memset", self.triU_f[:], 1.0, writes=["c_triU"])
        P.op(g, lambda e: e.affine_select(self.triU_f[:], self.triU_f[:], [[1, 128]], ALU.is_ge, 0.0,
                                          base=0, channel_multiplier=-1), writes=["c_triU"], reads=["c_triU"])
        P.v(g, "memset", self.maskLS[:], 1.0, writes=["c_maskLS"])
        P.op(g, lambda e: e.affine_select(self.maskLS[:], self.maskLS[:], [[1, 128]], ALU.is_ge, 0.0,
                                          base=0, channel_multiplier=-1), writes=["c_maskLS"], reads=["c_maskLS"])
        P.v(g, "memset", self.tri_bf[:], 1.0, writes=["c_tri"])
        P.op(g, lambda e: e.affine_select(self.tri_bf[:], self.tri_bf[:], [[-1, 128]], ALU.is_gt, 0.0,
                                          base=0, channel_multiplier=1), writes=["c_tri"], reads=["c_tri"])
        P.v(g, "memset", self.cmp_bf[:], 1.0, writes=["c_cmp"])
        P.op(g, lambda e: e.affine_select(self.cmp_bf[:], self.cmp_bf[:], [[1, 128]], ALU.is_ge, 0.0,
                                          base=0, channel_multiplier=-1), writes=["c_cmp"], reads=["c_cmp"])
        for m in range(4):
            P.v(g, "memset", self.mneg[:, m, :], 0.0, writes=[("c_mneg", m)])
            P.op(g, lambda e, m=m: e.affine_select(self.mneg[:, m, :], self.mneg[:, m, :], [[1, 512]], ALU.is_gt, -30000.0,
                                                   base=-128 * m, channel_multiplier=-1),
                 writes=[("c_mneg", m)], reads=[("c_mneg", m)])
        P.barrier()

    def col(self, name, i=0, n=1):
        o, _ = COL[name]
        return self.cols[:, o + i:o + i + n]

    def row_bc(self, name):
        o, n = ROW[name]
        return self.rows_in[0:1, o:o + n].broadcast_to([128, n])

    def norm_phase(self, src, gname):
        P = self.P
        with self.phase() as st:
            xt = [self.sb(st, "xt", [128, 8, 512], F32) for _ in range(2)]
            sq = [self.sb(st, "sq", [128, 8, 512], BF16) for _ in range(2)]
            ht = [self.sb(st, "ht", [128, 8, 512], BF16) for _ in range(2)]
            rs = [self.sb(st, "rs", [128, 512], F32) for _ in range(2)]
            pss = [self.ps(st, "pss") for _ in range(2)]
            for tt in range(NT):
                b = tt % 2
                ts = slice(tt * 512, (tt + 1) * 512)
                P.dma(SP, xt[b][:], src[:, ts].rearrange("(kc p) t -> p kc t", p=128), writes=[("xt", b)])
                P.act(sq[b][:], xt[b][:], AF.Square, reads=[("xt", b)], writes=[("sq", b)])
                for kc in range(8):
                    P.mm(pss[b][:], self.ones_bf[:], sq[b][:, kc, :], start=(kc == 0), stop=(kc == 7),
                         reads=[("sq", b)], writes=[("pss", b)])
                P.act(rs[b][:], pss[b][:], AF.Ln, bias=EPS, scale=1.0 / D, reads=[("pss", b)], writes=[("rs", b)])
                P.act(rs[b][:], rs[b][:], AF.Exp, scale=-0.5, reads=[("rs", b)], writes=[("rs", b)])
                for kc in range(8):
                    P.v(DVE, "scalar_tensor_tensor", ht[b][:, kc, :], xt[b][:, kc, :], self.col(gname, kc), rs[b][:],
                        ALU.mult, ALU.mult, reads=[("xt", b), ("rs", b)], writes=[("ht", b, kc)])
                P.dma(SP, self.hT[:, ts].rearrange("(kc p) t -> p kc t", p=128), ht[b][:],
                      reads=[("ht", b, kc) for kc in range(8)], writes=[("hT_d", tt)])

    def gemm_fm(self, in_d, KC, w_ap, col_groups, epilogue, pre_tile=None, post_tile=None, extra=None, npsum=6,
                norm=None, at_bufs=2):
        P = self.P
        allc = sorted({c for g in col_groups for c in g})
        cpos = {c: i for i, c in enumerate(allc)}
        with self.phase() as st:
            wsb = self.sb(st, "wsb", [128, KC, len(allc) * 128], BF16)
            nblk = 4 if len(col_groups) >= 8 else 1
            gpb = -(-len(col_groups) // nblk)
            col_key = {}
            for bi in range(0, len(col_groups), gpb):
                cs = sorted({c for g in col_groups[bi:bi + gpb] for c in g if c not in col_key})
                runs = []
                for c in cs:
                    if runs and runs[-1][1] == c:
                        runs[-1][1] = c + 128
                    else:
                        runs.append([c, c + 128])
                for (c0, c1) in runs:
                    assert cpos[c1 - 128] - cpos[c0] == (c1 - c0) // 128 - 1
                    p0 = cpos[c0] * 128
                    for kc in range(KC):
                        P.dma(POOL, wsb[:, kc, p0:p0 + (c1 - c0)], w_ap[kc * 128:(kc + 1) * 128, c0:c1],
                              writes=[("wsb", kc, c0)])
                    for c in range(c0, c1, 128):
                        col_key[c] = c0
            it = [self.sb(st, "it", [128, KC, 512], BF16) for _ in range(2)]
            pst = [self.ps(st, "pst") for _ in range(npsum)]
            if norm:
                n_src, n_g, n_store = norm
                nxt = self.sb(st, "nxt", [128, 8, 512], F32)
                nsq = self.sb(st, "nsq", [128, 8, 512], BF16)
                nrs = [self.sb(st, "nrs", [128, 512], F32) for _ in range(2)]
                npss = self.ps(st, "npss")
            ctx = extra(st) if extra else None
            pi = 0

            def prepare(tt):
                b = tt % 2
                ts = slice(tt * 512, (tt + 1) * 512)
                if norm:
                    P.dma(POOL, nxt[:], n_src[:, ts].rearrange("(kc p) t -> p kc t", p=128),
                          reads=[("xs_d", tt)], writes=["nxt"])
                    P.act(nsq[:], nxt[:], AF.Square, reads=["nxt"], writes=["nsq"])
                    for kc in range(8):
                        P.mm(npss[:], self.ones_bf[:], nsq[:, kc, :], start=(kc == 0), stop=(kc == 7),
                             reads=["nsq"], writes=["npss"])
                    P.act(nrs[b][:], npss[:], AF.Ln, bias=EPS, scale=1.0 / D, reads=["npss"], writes=[("nrs", b)])
                    P.act(nrs[b][:], nrs[b][:], AF.Exp, scale=-0.5, reads=[("nrs", b)], writes=[("nrs", b)])
                    for kc in range(8):
                        P.v(DVE, "scalar_tensor_tensor", it[b][:, kc, :], nxt[:, kc, :], self.col(n_g, kc),
                            nrs[b][:], ALU.mult, ALU.mult, reads=["nxt", ("nrs", b)], writes=[("it", b, kc)])
                    if n_store:
                        P.dma(SP, self.hT[:, ts].rearrange("(kc p) t -> p kc t", p=128), it[b][:],
                              reads=[("it", b, kc) for kc in range(8)], writes=[("hT_d", tt)])
                else:
                    P.dma(SP, it[b][:], in_d[:, ts].rearrange("(kc p) t -> p kc t", p=128),
                          reads=[("in_d", tt)], writes=[("it", b, kc) for kc in range(KC)])
                if pre_tile:
                    pre_tile(ctx, tt, ts)

            prepare(0)
            for tt in range(NT):
                b = tt % 2
                ts = slice(tt * 512, (tt + 1) * 512)
                if tt + 1 < NT:
                    prepare(tt + 1)
                for gi, g in enumerate(col_groups):
                    tiles, keys = [], []
                    for c in g:
                        pt = pst[pi % npsum]
                        key = ("pst", pi % npsum)
                        pi += 1
                        for kc in range(KC):
                            P.mm(pt[:], wsb[:, kc, cpos[c] * 128:(cpos[c] + 1) * 128], it[b][:, kc, :],
                                 start=(kc == 0), stop=(kc == KC - 1),
                                 reads=[("it", b, kc)] + ([("wsb", kc, col_key[c])] if (tt == 0) else []), writes=[key])
                        tiles.append(pt)
                        keys.append(key)
                    epilogue(ctx, tt, ts, gi, tiles, keys)
                if post_tile:
                    post_tile(ctx, tt, ts)

    def resid_gemm(self, in_d, KC, w_ap, src, dst):
        P = self.P

        def extra(st):
            return {"xt": [self.sb(st, "rxt", [128, 8, 512], F32) for _ in range(2)],
                    "xo": [self.sb(st, "rxo", [128, 8, 512], F32) for _ in range(2)]}

        def pre(ctx, tt, ts):
            b = tt % 2
            P.dma(SP, ctx["xt"][b][:], src[:, ts].rearrange("(kc p) t -> p kc t", p=128),
                  reads=[("xs_d", tt)], writes=[("rxt", b)])

        def epi(ctx, tt, ts, gi, tiles, keys):
            b = tt % 2
            P.v(DVE, "tensor_tensor", ctx["xo"][b][:, gi, :], tiles[0][:], ctx["xt"][b][:, gi, :], ALU.add,
                reads=[keys[0], ("rxt", b)], writes=[("rxo", b, gi)])

        def post(ctx, tt, ts):
            b = tt % 2
            o = P.dma(SP, dst[:, ts].rearrange("(kc p) t -> p kc t", p=128), ctx["xo"][b][:],
                      reads=[("rxo", b, gi) for gi in range(8)], writes=[("xs_d", tt)])
            self.final_ops.append(o)

        self.gemm_fm(in_d, KC, w_ap, [[c] for c in range(0, 1024, 128)], epi, pre, post, extra)

    def ffn(self, i, src, dst):
        P = self.P
        w_gu = self.W["ffn_w_gu"][i]

        def extra(st):
            at1 = self.sb(st, "at", [128, HC, 512], BF16)
            return {"sg": [self.sb(st, "sg", [128, 512], F32) for _ in range(3)],
                    "at": [at1, at1], "n": 0}

        def epi(ctx, tt, ts, gi, tiles, keys):
            b = tt % 2
            k = ctx["n"] % 3
            ctx["n"] += 1
            P.act(ctx["sg"][k][:], tiles[0][:], AF.Silu, reads=[keys[0]], writes=[("sg", k)])
            P.v(DVE, "tensor_tensor", ctx["at"][b][:, gi, :], ctx["sg"][k][:], tiles[1][:], ALU.mult,
                reads=[("sg", k), keys[1]], writes=[("at", 0, gi)])
            if gi == 10:
                P.dma(SP, self.actT[0:11 * 128, ts].rearrange("(kc p) t -> p kc t", p=128), ctx["at"][b][:, 0:11, :],
                      reads=[("at", 0, g_) for g_ in range(11)], writes=[("act_d", tt, 0)])

        def post(ctx, tt, ts):
            b = tt % 2
            P.dma(SP, self.actT[11 * 128:HC * 128, ts].rearrange("(kc p) t -> p kc t", p=128), ctx["at"][b][:, 11:HC, :],
                  reads=[("at", 0, gi) for gi in range(11, HC)], writes=[("act_d", tt, 1)])

        groups = [[j * 128, FFN_H + j * 128] for j in range(HC)]
        self.gemm_fm(self.hT, 8, w_gu, groups, epi, None, post, extra, norm=(src, f"ffnn{i}", False))
        self.resid_gemm(self.actT, HC, self.W["ffn_w_down"][i], src, dst)


    def attn_layer(self, i, j, src, dst):
        P = self.P
        w_qkv = self.W["sb_w_qkv"][j]

        def extra(st):
            return {"sq": [self.sb(st, "qsq", [128, 512], BF16) for _ in range(3)],
                    "r": [self.sb(st, "qr", [128, 512], F32) for _ in range(3)],
                    "ps2": [self.ps(st, "qps2") for _ in range(3)],
                    "qo": [self.sb(st, "qo", [128, 16, 512], BF16) for _ in range(2)], "n": 0}

        def epi(ctx, tt, ts, gi, tiles, keys):
            b = tt % 2
            k = ctx["n"] % 3
            ctx["n"] += 1
            isq = gi < 8
            gcol = self.col(f"qg{j}" if isq else f"kg{j}")
            P.act(ctx["sq"][k][:], tiles[0][:], AF.Square, reads=[keys[0]], writes=[("qsq", k)])
            P.mm(ctx["ps2"][k][:], self.blk_bf[:], ctx["sq"][k][:], reads=[("qsq", k)], writes=[("qps2", k)])
            if ctx.get("pend"):
                ctx.pop("pend")()

            def fin(k=k, b=b, gi=gi, isq=isq, gcol=gcol, t0=tiles[0], k0=keys[0]):
                P.act(ctx["r"][k][:], ctx["ps2"][k][:], AF.Ln, bias=EPS, scale=1.0 / 64,
                      reads=[("qps2", k)], writes=[("qr", k)])
                P.act(ctx["r"][k][:], ctx["r"][k][:], AF.Exp, scale=-0.5, bias=(-2.0794415416798357 if isq else 0.0),
                      reads=[("qr", k)], writes=[("qr", k)])
                P.v(DVE, "scalar_tensor_tensor", ctx["qo"][b][:, gi, :], t0[:], gcol, ctx["r"][k][:],
                    ALU.mult, ALU.mult, reads=[k0, ("qr", k)], writes=[("qo", b, gi)])
            ctx["pend"] = fin

        def post(ctx, tt, ts):
            b = tt % 2
            if ctx.get("pend"):
                ctx.pop("pend")()
            P.dma(SP, self.qT[:, ts].rearrange("(kc p) t -> p kc t", p=128), ctx["qo"][b][:, 0:8, :],
                  reads=[("qo", b, gi) for gi in range(8)], writes=[("qT_d", tt)])
            P.dma(SP, self.kT[:, ts].rearrange("(kc p) t -> p kc t", p=128), ctx["qo"][b][:, 8:16, :],
                  reads=[("qo", b, gi) for gi in range(8, 16)], writes=[("kT_d", tt)])

        self.gemm_fm(self.hT, 8, w_qkv, [[c] for c in range(0, 2048, 128)], epi, None, post, extra, npsum=4,
                     norm=(src, f"mixn{i}", True))

        with self.phase() as st:
            wv = self.sb(st, "wv", [128, 8, 1024], BF16)
            self.load_w(wv, "wv", w_qkv, 2048, 3072, 8)
            wk = [("wv", kc) for kc in range(8)]
            it = [self.sb(st, "vit", [128, 8, 512], BF16) for _ in range(2)]
            vo = [self.sb(st, "vo", [128, 4, 1024], BF16) for _ in range(2)]
            pv = [self.ps(st, "pv") for _ in range(4)]
            pi = 0
            for tt in range(NT):
                b = tt % 2
                ts = slice(tt * 512, (tt + 1) * 512)
                P.dma(SP, it[b][:], self.hT[:, ts].rearrange("(kc p) t -> p kc t", p=128), writes=[("vit", b)])
                for bl in range(4):
                    for cs in range(2):
                        pt = pv[pi % 4]; key = ("pv", pi % 4); pi += 1
                        for kc in range(8):
                            P.mm(pt[:], it[b][:, kc, bl * 128:(bl + 1) * 128], wv[:, kc, cs * 512:(cs + 1) * 512],
                                 start=(kc == 0), stop=(kc == 7),
                                 reads=[("vit", b)] + (wk if tt == 0 else []), writes=[key])
                        eng = ACT if (pi % 2) else DVE
                        if eng == ACT:
                            P.act(vo[b][:, bl, cs * 512:(cs + 1) * 512], pt[:], AF.Copy, reads=[key],
                                  writes=[("vo", b, bl, cs)])
                        else:
                            P.v(DVE, "tensor_copy", vo[b][:, bl, cs * 512:(cs + 1) * 512], pt[:], reads=[key],
                                writes=[("vo", b, bl, cs)])
                P.dma(SP, self.vtok[ts, :].rearrange("(bl p) c -> p bl c", p=128), vo[b][:],
                      reads=[("vo", b, bl, cs) for bl in range(4) for cs in range(2)], writes=[("v_d", tt)])

        self.attn_core()
        self.resid_gemm(self.oT, 8, self.W["sb_w_o"][j], src, dst)

    def attn_core(self):
        P = self.P
        with self.phase() as st:
            q2 = [self.sb(st, "q2", [128, T], BF16) for _ in range(2)]
            k2 = [self.sb(st, "k2", [128, T], BF16) for _ in range(2)]
            v2 = [self.sb(st, "v2", [128, 32, 128], BF16) for _ in range(2)]
            NB = 4
            e_ = [self.sb(st, "ae", [128, 512], F32) for _ in range(NB)]
            sp_ = [self.sb(st, "asp", [128, 512], F32) for _ in range(NB)]
            nsp = [self.sb(st, "ansp", [128, 512], BF16) for _ in range(NB)]
            t1 = [self.sb(st, "at1", [128, 512], F32) for _ in range(NB)]
            t2 = [self.sb(st, "at2", [128, 512], F32) for _ in range(NB)]
            w_ = [self.sb(st, "aw", [128, 512], BF16) for _ in range(NB)]
            ob = [self.sb(st, "aob", [64, 512], BF16) for _ in range(2)]
            Z = [self.ps(st, "aZ") for _ in range(2)]
            S = [self.ps(st, "aS") for _ in range(2)]
            O = [self.ps(st, "aO") for _ in range(2)]

            blocks = []
            for jp in range(self.attn_pairs):
                for hh in range(2):
                    for qt in range(NT):
                        nk = 4 * qt + 4
                        for n, kb in enumerate(reversed(range(nk))):
                            blocks.append((jp, hh, qt, kb, n, nk))
            NBLK = len(blocks)

            def load_pair(jp):
                b = jp % 2
                rs_ = slice(jp * 128, (jp + 1) * 128)
                P.dma(SP, q2[b][:], self.qT[rs_, :], writes=[("q2", b)])
                P.dma(SP, k2[b][:], self.kT[rs_, :], writes=[("k2", b)])
                P.dma(SP, v2[b][:], self.vtok[:, rs_].rearrange("(bl p) c -> p bl c", p=128), writes=[("v2", b)])

            load_pair(0)
            qtile_id = {}
            cnt = 0
            for bi, (jp, hh, qt, kb, n, nk) in enumerate(blocks):
                if n == 0:
                    qtile_id[(jp, hh, qt)] = cnt
                    cnt += 1

            def stageA(i):
                jp, hh, qt, kb, n, nk = blocks[i]
                b = jp % 2
                hp = slice(hh * 64, (hh + 1) * 64)
                z = Z[i % 2]; zk = ("aZ", i % 2)
                k = i % NB
                m = kb - 4 * qt
                P.mm(z[:], k2[b][hp, kb * 128:(kb + 1) * 128], q2[b][hp, qt * 512:(qt + 1) * 512],
                     start=True, stop=(m < 0), reads=[("q2", b), ("k2", b)], writes=[zk])
                if m >= 0:
                    P.mm(z[:], self.ident[:], self.mneg[:, m, :], start=False, stop=True, writes=[zk])

            def stageA2(i):
                jp, hh, qt, kb, n, nk = blocks[i]
                z = Z[i % 2]; zk = ("aZ", i % 2)
                k = i % NB
                P.act(e_[k][:], z[:], AF.Exp, reads=[zk], writes=[("ae", k)])
                P.act(sp_[k][:], e_[k][:], AF.Ln, bias=1.0, reads=[("ae", k)], writes=[("asp", k)])
                P.v(DVE, "tensor_scalar", nsp[k][:], sp_[k][:], -1.0, None, ALU.mult,
                    reads=[("asp", k)], writes=[("ansp", k)])
                P.v(DVE, "tensor_tensor", t1[k][:], z[:], sp_[k][:], ALU.subtract,
                    reads=[zk, ("asp", k)], writes=[("at1", k)])

            def stageB(i):
                jp, hh, qt, kb, n, nk = blocks[i]
                k = i % NB
                kp = (i - 1) % NB
                X = S[n % 2]; Xk = ("aS", n % 2)
                Oo = S[(n + 1) % 2]; Ok = ("aS", (n + 1) % 2)
                P.mm(X[:], self.tri_bf[:], nsp[k][:], start=(n == 0), stop=True, reads=[("ansp", k)], writes=[Xk])
                P.v(DVE, "tensor_tensor", t2[k][:], X[:], t1[k][:], ALU.add,
                    reads=[Xk, ("at1", k)], writes=[("at2", k)])
                if n < nk - 1:
                    if n == 0:
                        P.mm(Oo[:], self.ones_bf[:], nsp[k][:], start=True, stop=True,
                             reads=[("ansp", k)], writes=[Ok])
                    else:
                        P.mm(Oo[:], self.cmp_bf[:], nsp[kp][:], start=False, stop=True,
                             reads=[("ansp", kp)], writes=[Ok])
                        P.mm(Oo[:], self.ones_bf[:], nsp[k][:], start=False, stop=True,
                             reads=[("ansp", k)], writes=[Ok])

            def stageC(i):
                jp, hh, qt, kb, n, nk = blocks[i]
                k = i % NB
                P.act(w_[k][:], t2[k][:], AF.Exp, reads=[("at2", k)], writes=[("aw", k)])

            def stageD(i):
                jp, hh, qt, kb, n, nk = blocks[i]
                k = i % NB
                b = jp % 2
                qid = qtile_id[(jp, hh, qt)]
                o = O[qid % 2]; ok = ("aO", qid % 2)
                P.mm(o[0:64, :], v2[b][:, kb, hh * 64:(hh + 1) * 64], w_[k][:], start=(n == 0), stop=(n == nk - 1),
                     reads=[("v2", b), ("aw", k)], writes=[ok])
                if n == nk - 1:
                    ob_ = ob[qid % 2]; obk = ("aob", qid % 2)
                    P.v(DVE, "tensor_copy", ob_[:], o[0:64, :], reads=[ok], writes=[obk])
                    r0 = (jp * 2 + hh) * 64
                    P.dma(SP, self.oT[r0:r0 + 64, qt * 512:(qt + 1) * 512], ob_[:], reads=[obk],
                          writes=[("oT_d", jp, hh, qt)])

            P.relax = True
            for i in range(NBLK + 3):
                if i < NBLK:
                    stageA(i)
                if 0 <= i - 1 < NBLK:
                    stageB(i - 1)
                if i < NBLK:
                    stageA2(i)
                if 0 <= i - 2 < NBLK:
                    stageC(i - 2)
                if 0 <= i - 3 < NBLK:
                    stageD(i - 3)
                    jp_, hh_, qt_, kb_, n_, nk_ = blocks[i - 3]
                    if hh_ == 0 and qt_ == 0 and n_ == 0 and jp_ + 1 < self.attn_pairs:
                        load_pair(jp_ + 1)
            P.relax = False

    def gmlp_layer(self, i, src, dst):
        P = self.P
        w_in = self.W["gm_w_in"][0]

        def extra(st):
            return {"uo": [self.sb(st, "uo", [128, 16, 512], BF16) for _ in range(2)]}

        def epi(ctx, tt, ts, gi, tiles, keys):
            b = tt % 2
            P.act(ctx["uo"][b][:, gi, :], tiles[0][:], AF.Gelu, bias=self.col("gm_bu", gi),
                  reads=[keys[0]], writes=[("uo", b, gi)])

        def post(ctx, tt, ts):
            b = tt % 2
            P.dma(SP, self.uT[:, ts].rearrange("(kc p) t -> p kc t", p=128), ctx["uo"][b][:],
                  reads=[("uo", b, gi) for gi in range(16)], writes=[("uT_d", tt)])

        self.gemm_fm(self.hT, 8, w_in, [[c] for c in range(0, 2048, 128)], epi, None, post, extra,
                     norm=(src, f"mixn{i}", True))

        with self.phase() as st:
            wv = self.sb(st, "gwv", [128, 8, 2048], BF16)
            self.load_w(wv, "gwv", w_in, 2048, 4096, 8)
            wk = [("gwv", kc) for kc in range(8)]
            wsf = self.sb(st, "wsf", [128, 16, 128], F32)
            wsb_ = self.sb(st, "wsb_", [128, 16, 128], BF16)
            P.dma(SP, wsf[:], self.wsT_in, writes=["wsf"])
            for g in range(16):
                P.v(POOL, "tensor_tensor", wsb_[:, g, :], wsf[:, g, :], self.maskLS[:], ALU.mult,
                    reads=["wsf"], writes=[("wsb_", g)])
            wsk = [("wsb_", g) for g in range(16)]
            bv = self.sb(st, "gbv", [128, 2048], F32)
            bs = self.sb(st, "gbs", [128, 16, 128], F32)
            P.dma(SP, bv[:], self.row_bc("gm_bv"), writes=["gbv"])
            P.dma(SP, bs[:].rearrange("p g t -> p (g t)"), self.row_bc("gm_bs"), writes=["gbs"])
            it = [self.sb(st, "git", [128, 8, 512], BF16) for _ in range(2)]
            ut = [self.sb(st, "gut", [128, 16, 512], BF16) for _ in range(2)]
            po = [self.sb(st, "gpo", [128, 16, 512], BF16) for _ in range(2)]
            gv = [self.sb(st, "ggv", [128, 2048], F32) for _ in range(2)]
            vs = [self.sb(st, "gvs", [128, 2048], BF16) for _ in range(2)]
            junk = self.sb(st, "gjunk", [128, 512], F32)
            ss = [self.sb(st, "gss", [128, 4], F32) for _ in range(2)]
            rr = [self.sb(st, "grr", [128, 1], F32) for _ in range(2)]
            tm = [self.sb(st, "gtm", [128, 4, 128], F32) for _ in range(2)]
            pv = [self.ps(st, "gpv") for _ in range(4)]
            pm = [self.ps(st, "gpm") for _ in range(2)]
            pi = 0
            mi = 0
            blk = 0
            for tt in range(NT):
                b = tt % 2
                ts = slice(tt * 512, (tt + 1) * 512)
                P.dma(SP, it[b][:], self.hT[:, ts].rearrange("(kc p) t -> p kc t", p=128), writes=[("git", b)])
                P.dma(SP, ut[b][:], self.uT[:, ts].rearrange("(kc p) t -> p kc t", p=128), writes=[("gut", b)])
                for bl in range(4):
                    c2 = blk % 2
                    blk += 1
                    bsl = slice(bl * 128, (bl + 1) * 128)
                    for cs in range(4):
                        pt = pv[pi % 4]; key = ("gpv", pi % 4); pi += 1
                        csl = slice(cs * 512, (cs + 1) * 512)
                        for kc in range(8):
                            P.mm(pt[:], it[b][:, kc, bsl], wv[:, kc, csl], start=(kc == 0), stop=(kc == 7),
                                 reads=[("git", b)] + (wk if tt == 0 else []), writes=[key])
                        P.v(DVE, "tensor_tensor", gv[c2][:, csl], pt[:], bv[:, csl], ALU.add,
                            reads=[key, "gbv"], writes=[("ggv", c2, cs)])
                        P.act(gv[c2][:, csl], gv[c2][:, csl], AF.Gelu, reads=[("ggv", c2, cs)], writes=[("ggv", c2, cs)])
                        P.act(junk[:], gv[c2][:, csl], AF.Square, accum_out=ss[c2][:, cs:cs + 1],
                              reads=[("ggv", c2, cs)], writes=["gjunk", ("gss", c2, cs)])
                    P.v(DVE, "tensor_reduce", rr[c2][:], ss[c2][:], mybir.AxisListType.X, ALU.add,
                        reads=[("gss", c2, cs) for cs in range(4)], writes=[("grr", c2)])
                    P.act(rr[c2][:], rr[c2][:], AF.Sqrt, bias=EPS, scale=1.0 / 2048, reads=[("grr", c2)], writes=[("grr", c2)])
                    P.v(DVE, "reciprocal", rr[c2][:], rr[c2][:], reads=[("grr", c2)], writes=[("grr", c2)])
                    P.v(DVE, "tensor_scalar", vs[c2][:], gv[c2][:], rr[c2][:, 0:1], None, ALU.mult,
                        reads=[("ggv", c2, cs) for cs in range(4)] + [("grr", c2)], writes=[("gvs", c2)])
                    for g4 in range(4):
                        pmt = pm[mi % 2]; pmk = ("gpm", mi % 2)
                        tmt = tm[mi % 2]; tmk = ("gtm", mi % 2)
                        mi += 1
                        for gg in range(4):
                            g = g4 * 4 + gg
                            P.mm(pmt[:, gg * 128:(gg + 1) * 128], vs[c2][:, g * 128:(g + 1) * 128], wsb_[:, g, :],
                                 reads=[("gvs", c2)] + (wsk if blk == 1 else []), writes=[pmk])
                        gsl = slice(g4 * 4, g4 * 4 + 4)
                        o_, _ = COL["gm_vg"]
                        gain_bc = self.cols[:, o_ + g4 * 4:o_ + g4 * 4 + 4].unsqueeze(2).broadcast_to([128, 4, 128])
                        P.v(DVE, "tensor_tensor", tmt[:], pmt[:].rearrange("p (g t) -> p g t", g=4), gain_bc, ALU.mult,
                            reads=[pmk], writes=[tmk])
                        P.v(DVE, "tensor_tensor", tmt[:], tmt[:], bs[:, gsl, :], ALU.add,
                            reads=[tmk, "gbs"], writes=[tmk])
                        P.v(DVE, "tensor_tensor", po[b][:, gsl, bsl], tmt[:], ut[b][:, gsl, bsl], ALU.mult,
                            reads=[tmk, ("gut", b)], writes=[("gpo", b, bl, g4)])
                P.dma(SP, self.prodT[:, ts].rearrange("(kc p) t -> p kc t", p=128), po[b][:],
                      reads=[("gpo", b, bl, g4) for bl in range(4) for g4 in range(4)], writes=[("prod_d", tt)])

        self.resid_gemm(self.prodT, 16, self.W["gm_w_out"][0], src, dst)

    def ssd_layer(self, i, src, dst):
        P = self.P
        w_in = self.W["ssm_w_in"][0]

        for cp in range(4):
            def extra(st):
                return {"raw": [self.sb(st, "craw", [128, 515], F32) for _ in range(8)],
                        "acc": [self.sb(st, "cacc", [128, 512], F32) for _ in range(3)],
                        "xo": [self.sb(st, "cxo", [128, 8, 512], BF16) for _ in range(2)], "n": 0}

            def epi(ctx, tt, ts, gi, tiles, keys, cp=cp):
                b = tt % 2
                ch = cp * 8 + gi
                raw = ctx["raw"][gi]; rk = ("craw", gi)
                k = ctx["n"] % 3
                ctx["n"] += 1
                acc = ctx["acc"][k]; ak = ("cacc", k)
                if tt == 0:
                    P.v(POOL, "memset", raw[:, 0:3], 0.0, writes=[rk])
                P.act(raw[:, 3:515], tiles[0][:], AF.Copy, reads=[keys[0], rk], writes=[rk])
                o_, _ = COL["conv_w"]
                wc = lambda kk: self.cols[:, o_ + ch * 4 + kk:o_ + ch * 4 + kk + 1]
                P.act(acc[:], raw[:, 0:512], AF.Identity, bias=self.col("conv_b", ch), scale=wc(0),
                      reads=[rk], writes=[ak])
                P.relax = True
                for kk in range(1, 4):
                    P.v(DVE, "scalar_tensor_tensor", acc[:], raw[:, kk:kk + 512], wc(kk), acc[:], ALU.mult, ALU.add,
                        reads=[rk, ak], writes=[ak])
                P.relax = False
                if ctx.get("pend"):
                    ctx.pop("pend")()

                def fin(acc=acc, ak=ak, raw=raw, rk=rk, b=b, gi=gi):
                    P.act(ctx["xo"][b][:, gi, :], acc[:], AF.Silu, reads=[ak], writes=[("cxo", b, gi)])
                    P.v(POOL, "tensor_copy", raw[:, 0:3], raw[:, 512:515], reads=[rk, ak], writes=[rk])
                ctx["pend"] = fin

            def post(ctx, tt, ts, cp=cp):
                b = tt % 2
                if ctx.get("pend"):
                    ctx.pop("pend")()
                P.dma(SP, self.xbcT[cp * 1024:(cp + 1) * 1024, ts].rearrange("(kc p) t -> p kc t", p=128),
                      ctx["xo"][b][:], reads=[("cxo", b, gi) for gi in range(8)], writes=[("xbc_d", cp, tt)])

            self.gemm_fm(self.hT, 8, w_in, [[2048 + cp * 1024 + c] for c in range(0, 1024, 128)],
                         epi, None, post, extra, norm=((src, f"mixn{i}", True) if cp == 0 else None))

        self.ssd_core(w_in)
        self.resid_gemm(self.ynT, 16, self.W["ssm_w_out"][0], src, dst)

    def ssd_core(self, w_in):
        P = self.P
        AX = mybir.AxisListType.X
        with self.phase() as st:
            wz = self.sb(st, "wz", [128, 8, 2048], BF16)
            self.load_w(wz, "wz", w_in, 0, 2048, 8)
            wdt = self.sb(st, "wdt", [128, 8, 32], BF16)
            self.load_w(wdt, "wdt", w_in, 6144, 6176, 8)
            wzk = [("wz", kc) for kc in range(8)]
            wdk = [("wdt", kc) for kc in range(8)]
            dtb = self.sb(st, "dtb", [128, 32], F32)
            A_ = self.sb(st, "A_", [128, 32], F32)
            Dk = self.sb(st, "Dk", [128, 32], F32)
            ng = self.sb(st, "ng", [128, 2048], F32)
            P.dma(SP, dtb[:], self.row_bc("dt_bias"), writes=["dtb"])
            P.dma(SP, A_[:], self.row_bc("a_log"), writes=["A_"])
            P.dma(SP, Dk[:], self.row_bc("d_skip"), writes=["Dk"])
            P.dma(SP, ng[:], self.row_bc("ssm_ng"), writes=["ng"])
            P.act(A_[:], A_[:], AF.Exp, reads=["A_"], writes=["A_"])
            P.v(DVE, "tensor_scalar", A_[:], A_[:], -1.0, None, ALU.mult, reads=["A_"], writes=["A_"])
            prev = self.sb(st, "prev", [128, 8, 256], F32)
            prevb = self.sb(st, "prevb", [128, 8, 256], BF16)
            P.v(POOL, "memset", prev[:], 0.0, writes=[("prev", g) for g in range(8)])
            P.v(POOL, "memset", prevb[:], 0.0, writes=[("prevb", g) for g in range(8)])
            it = [self.sb(st, "sit", [128, 8, 512], BF16) for _ in range(2)]
            xb = [self.sb(st, "sxb", [128, 32, 128], BF16) for _ in range(2)]
            ynT_t = [self.sb(st, "synT", [128, 16, 512], BF16) for _ in range(2)]
            sm = lambda n: self.sb(st, n, [128, 32], F32)
            dtr, dmx, dmn, dte, dt_, a_, acum, dd, dtdec, eac, cd = [sm(f"s{n}") for n in
                ("dtr", "dmx", "dmn", "dte", "dt", "a", "acum", "dd", "dtdec", "eac", "cd")]
            nacum = sm("snacum")
            ahi = self.sb(st, "sahi", [128, 32], BF16)
            alo = self.sb(st, "salo", [128, 32], BF16)
            sz = self.sb(st, "ssz", [128, 2048], BF16)
            Xt = self.sb(st, "sXt", [128, 2048], BF16)
            Bt = self.sb(st, "sBt", [128, 8, 128], BF16)
            xdt = self.sb(st, "sxdt", [128, 2048], BF16)
            xdd = self.sb(st, "sxdd", [128, 2048], BF16)
            ytok = self.sb(st, "sytok", [128, 2048], F32)
            tmpx = self.sb(st, "stmpx", [128, 2048], F32)
            yn = self.sb(st, "syn", [128, 2048], BF16)
            cbm = [self.sb(st, "scbm", [128, 128], F32) for _ in range(3)]
            seg = [self.sb(st, "sseg", [128, 4, 128], F32) for _ in range(3)]
            Mm = [self.sb(st, "sM", [128, 4, 128], BF16) for _ in range(3)]
            nmask = self.sb(st, "snmask", [128, 4, 128], BF16)
            P.v(POOL, "memset", nmask[:], 0.0, writes=["snmask"])
            P.op(POOL, lambda e: e.affine_select(nmask[:], nmask[:], [[0, 4], [1, 128]], ALU.is_ge, -30000.0,
                                                 base=0, channel_multiplier=-1), reads=["snmask"], writes=["snmask"])
            yo = [self.sb(st, "syo", [128, 256], F32) for _ in range(2)]
            ptmp = [self.sb(st, "sptmp", [128, 256], F32) for _ in range(2)]
            junk = self.sb(st, "sjunk", [128, 256], F32)
            ssq = self.sb(st, "sssq", [128, 8], F32)
            rstd = self.sb(st, "srstd", [128, 8], F32)
            pzt = [self.ps(st, "spz") for _ in range(2)]
            pab = [self.ps(st, "spab") for _ in range(2)]
            pys = [self.ps(st, "spy") for _ in range(2)]
            pcss = [self.ps(st, "spcs") for _ in range(2)]
            psm = pcss[1]

            hv = lambda t: t[:].rearrange("p (h d) -> p h d", d=64)
            bc64 = lambda ap_: ap_.unsqueeze(2).broadcast_to([128, ap_.shape[1], 64])
            zi = 0
            ti = 0
            hi = 0
            for tt in range(NT):
                b = tt % 2
                ts = slice(tt * 512, (tt + 1) * 512)
                P.dma(SP, it[b][:], self.hT[:, ts].rearrange("(kc p) t -> p kc t", p=128), writes=[("sit", b)])
                for bl in range(4):
                    c = tt * 4 + bl
                    xbb = xb[c % 2]; xk = ("sxb", c % 2)
                    csl = slice(c * 128, (c + 1) * 128)
                    bsl = slice(bl * 128, (bl + 1) * 128)
                    P.dma(SP, xbb[:], self.xbcT[:, csl].rearrange("(kc p) t -> p kc t", p=128), writes=[xk])
                    first = (c == 0)
                    for kc in range(8):
                        P.mm(psm[:, 0:32], it[b][:, kc, bsl], wdt[:, kc, :], start=(kc == 0), stop=(kc == 7),
                             reads=[("sit", b)] + (wdk if first else []), writes=[("spcs", 1)])
                    P.v(DVE, "tensor_tensor", dtr[:], psm[:, 0:32], dtb[:], ALU.add, reads=[("spcs", 1), "dtb"], writes=["dtr"])
                    P.v(DVE, "tensor_scalar", dmx[:], dtr[:], 0.0, None, ALU.max, reads=["dtr"], writes=["dmx"])
                    P.v(DVE, "tensor_scalar", dmn[:], dtr[:], 0.0, None, ALU.min, reads=["dtr"], writes=["dmn"])
                    P.v(DVE, "tensor_tensor", dmn[:], dmn[:], dmx[:], ALU.subtract, reads=["dmn", "dmx"], writes=["dmn"])
                    P.act(dte[:], dmn[:], AF.Exp, reads=["dmn"], writes=["dte"])
                    P.act(dte[:], dte[:], AF.Ln, bias=1.0, reads=["dte"], writes=["dte"])
                    P.v(DVE, "tensor_tensor", dt_[:], dmx[:], dte[:], ALU.add, reads=["dmx", "dte"], writes=["dt"])
                    P.v(DVE, "tensor_tensor", a_[:], dt_[:], A_[:], ALU.mult, reads=["dt", "A_"], writes=["a"])
                    P.v(DVE, "tensor_copy", ahi[:], a_[:], reads=["a"], writes=["ahl"])
                    P.v(DVE, "tensor_tensor", alo[:], a_[:], ahi[:], ALU.subtract, reads=["a", "ahl"], writes=["ahl"])
                    P.mm(psm[:, 32:64], self.triU_f[:], a_[:], reads=["a"], writes=[("spcs", 1)])
                    P.mm(psm[:, 64:96], self.ones_f[:], a_[:], reads=["a"], writes=[("spcs", 1)])
                    P.act(acum[:], psm[:, 32:64], AF.Copy, reads=[("spcs", 1)], writes=["acum"])
                    P.act(nacum[:], psm[:, 32:64], AF.Copy, scale=-1.0, reads=[("spcs", 1)], writes=["nacum"])
                    P.v(DVE, "tensor_tensor", dd[:], psm[:, 64:96], acum[:], ALU.subtract,
                        reads=[("spcs", 1), "acum"], writes=["dd"])
                    P.act(dd[:], dd[:], AF.Exp, reads=["dd"], writes=["dd"])
                    P.v(DVE, "tensor_tensor", dtdec[:], dt_[:], dd[:], ALU.mult, reads=["dt", "dd"], writes=["dtdec"])
                    P.act(eac[:], acum[:], AF.Exp, reads=["acum"], writes=["eac"])
                    P.act(cd[:], psm[:, 64:96], AF.Exp, reads=[("spcs", 1)], writes=["cd"])
                    if CUT <= 1:
                        continue
                    for cs in range(4):
                        pt = pzt[zi % 2]; zk = ("spz", zi % 2); zi += 1
                        for kc in range(8):
                            P.mm(pt[:], it[b][:, kc, bsl], wz[:, kc, cs * 512:(cs + 1) * 512],
                                 start=(kc == 0), stop=(kc == 7),
                                 reads=[("sit", b)] + (wzk if first else []), writes=[zk])
                        P.act(sz[:, cs * 512:(cs + 1) * 512], pt[:], AF.Silu, reads=[zk], writes=[("ssz", cs)])
                    if CUT <= 2:
                        continue
                    for q4 in range(6):
                        pt = pzt[zi % 2]; tk = ("spz", zi % 2); zi += 1
                        for u in range(4):
                            ch = q4 * 4 + u
                            P.mm(pt[:, u * 128:(u + 1) * 128], xbb[:, ch, :], self.ident[:],
                                 reads=[xk], writes=[tk])
                        if q4 < 4:
                            P.act(Xt[:, q4 * 512:(q4 + 1) * 512], pt[:], AF.Copy, reads=[tk], writes=[("sXt", q4)])
                        else:
                            P.act(Bt[:, (q4 - 4) * 4:(q4 - 4) * 4 + 4, :].rearrange("p g n -> p (g n)"), pt[:], AF.Copy,
                                  reads=[tk], writes=[("sBt", q4 - 4)])
                    xtk = [("sXt", q4) for q4 in range(4)]
                    if CUT <= 2.5:
                        continue
                    P.v(DVE, "tensor_tensor", hv(xdt), hv(Xt), bc64(dt_[:, :]), ALU.mult,
                        reads=xtk + ["dt"], writes=["sxdt"])
                    P.v(DVE, "tensor_tensor", hv(xdd), hv(Xt), bc64(dtdec[:, :]), ALU.mult,
                        reads=xtk + ["dtdec"], writes=["sxdd"])
                    P.v(POOL, "tensor_tensor", hv(tmpx), hv(Xt), bc64(Dk[:, :]), ALU.mult,
                        reads=xtk + ["Dk", "sxdd"], writes=["stmpx"])
                    if CUT <= 3:
                        continue
                    def st1(g):
                        cb_ = cbm[g % 3]; cbk = ("scbm", g % 3)
                        pcs = pcss[g % 2]; pck = ("spcs", g % 2)
                        pa = pab[g % 2]; pak = ("spab", g % 2)
                        sg_ = seg[g % 3]; sgk = ("sseg", g % 3)
                        BTg = xbb[:, 16 + g, :]
                        CTg = xbb[:, 24 + g, :]
                        P.mm(pcs[:, 0:128], BTg, CTg, reads=[xk], writes=[pck])
                        P.v(DVE, "tensor_tensor", cb_[:], pcs[:, 0:128], self.maskLS[:], ALU.mult,
                            reads=[pck], writes=[cbk])
                        for r in range(4):
                            h = 4 * g + r
                            P.mm(pa[:, r * 128:(r + 1) * 128], ahi[:, h:h + 1].broadcast_to([128, 128]), self.triU_bf[:],
                                 start=True, stop=False, reads=["ahl"], writes=[pak])
                            P.mm(pa[:, r * 128:(r + 1) * 128], alo[:, h:h + 1].broadcast_to([128, 128]), self.triU_bf[:],
                                 start=False, stop=True, reads=["ahl"], writes=[pak])
                        P.v(DVE, "scalar_tensor_tensor", sg_[:], pa[:].rearrange("p (r l) -> p r l", r=4), 0.0,
                            acum[:, 4 * g:4 * g + 4].unsqueeze(2).broadcast_to([128, 4, 128]), ALU.add, ALU.min,
                            reads=[pak, "acum"], writes=[sgk])
                        for r in range(4):
                            h = 4 * g + r
                            P.act(sg_[:, r, :], sg_[:, r, :], AF.Exp, bias=nacum[:, h:h + 1],
                                  reads=[sgk, "nacum"], writes=[sgk])

                    def st2(g):
                        cb_ = cbm[g % 3]; cbk = ("scbm", g % 3)
                        pcs = pcss[g % 2]; pck = ("spcs", g % 2)
                        sg_ = seg[g % 3]; sgk = ("sseg", g % 3)
                        M_ = Mm[g % 3]; Mk = ("sM", g % 3)
                        py = pys[g % 2]; pyk = ("spy", g % 2)
                        CTg = xbb[:, 24 + g, :]
                        P.v(DVE, "tensor_tensor", M_[:], sg_[:], cb_[:].unsqueeze(1).broadcast_to([128, 4, 128]), ALU.mult,
                            reads=[sgk, cbk], writes=[Mk])
                        for r in range(4):
                            h = 4 * g + r
                            P.mm(py[:, r * 64:(r + 1) * 64], M_[:, r, :], xdt[:, h * 64:(h + 1) * 64],
                                 reads=[Mk, "sxdt"], writes=[pyk])
                        P.mm(py[:, 256:512], CTg, prevb[:, g, :], reads=[xk, ("prevb", g)], writes=[pyk])
                        P.mm(pcs[:, 128:384], Bt[:, g, :], xdd[:, g * 256:(g + 1) * 256],
                             reads=[("sBt", g // 4), "sxdd"], writes=[pck])
                        pt_ = ptmp[g % 2]; ptk = ("sptmp", g % 2)
                        P.v(POOL, "tensor_tensor", pt_[:].rearrange("p (h d) -> p h d", d=64),
                            prev[:, g, :].rearrange("p (h d) -> p h d", d=64), bc64(cd[:, 4 * g:4 * g + 4]), ALU.mult,
                            reads=[("prev", g), "cd"], writes=[ptk])

                    def st3(g):
                        pcs = pcss[g % 2]; pck = ("spcs", g % 2)
                        py = pys[g % 2]; pyk = ("spy", g % 2)
                        yo_ = yo[g % 2]; yok = ("syo", g % 2)
                        pt_ = ptmp[g % 2]; ptk = ("sptmp", g % 2)
                        P.v(DVE, "tensor_tensor", yo_[:].rearrange("p (h d) -> p h d", d=64),
                            py[:, 256:512].rearrange("p (h d) -> p h d", d=64), bc64(eac[:, 4 * g:4 * g + 4]), ALU.mult,
                            reads=[pyk, "eac"], writes=[yok])
                        P.v(DVE, "tensor_tensor", ytok[:, g * 256:(g + 1) * 256], py[:, 0:256], yo_[:], ALU.add,
                            reads=[pyk, yok], writes=[("sytok", g)])
                        P.v(DVE, "tensor_tensor", prev[:, g, :], pt_[:], pcs[:, 128:384], ALU.add,
                            reads=[ptk, pck], writes=[("prev", g)])
                        P.v(POOL, "tensor_copy", prevb[:, g, :], prev[:, g, :], reads=[("prev", g)], writes=[("prevb", g)])

                    for step in range(8 + 2):
                        if step < 8:
                            st1(step)
                        if 0 <= step - 1 < 8:
                            st2(step - 1)
                        if 0 <= step - 2 < 8:
                            st3(step - 2)
                    if CUT <= 4:
                        continue
                    yk = [("sytok", g) for g in range(8)]
                    P.v(DVE, "tensor_tensor", ytok[:], ytok[:], tmpx[:], ALU.add, reads=yk + ["stmpx"], writes=yk)
                    P.v(DVE, "tensor_tensor", ytok[:], ytok[:], sz[:], ALU.mult,
                        reads=yk + [("ssz", cs) for cs in range(4)], writes=yk)
                    for g in range(8):
                        P.act(junk[:], ytok[:, g * 256:(g + 1) * 256], AF.Square, accum_out=ssq[:, g:g + 1],
                              reads=[("sytok", g)], writes=["sjunk", ("sssq", g)])
                    P.act(rstd[:], ssq[:], AF.Sqrt, bias=EPS, scale=1.0 / 256,
                          reads=[("sssq", g) for g in range(8)], writes=["srstd"])
                    P.v(DVE, "reciprocal", rstd[:], rstd[:], reads=["srstd"], writes=["srstd"])
                    P.v(DVE, "tensor_tensor", ytok[:].rearrange("p (g d) -> p g d", d=256),
                        ytok[:].rearrange("p (g d) -> p g d", d=256),
                        rstd[:, :].unsqueeze(2).broadcast_to([128, 8, 256]), ALU.mult,
                        reads=yk + ["srstd"], writes=yk)
                    P.v(DVE, "tensor_tensor", yn[:], ytok[:], ng[:], ALU.mult, reads=yk + ["ng"], writes=["syn"])
                    for q4 in range(4):
                        pt = pzt[zi % 2]; tk = ("spz", zi % 2); zi += 1
                        for u in range(4):
                            ch = q4 * 4 + u
                            P.mm(pt[:, u * 128:(u + 1) * 128], yn[:, ch * 128:(ch + 1) * 128], self.ident[:],
                                 reads=["syn"], writes=[tk])
                        P.act(ynT_t[b][:, q4 * 4:q4 * 4 + 4, bsl], pt[:].rearrange("p (u t) -> p u t", u=4), AF.Copy,
                              reads=[tk], writes=[("synT", b, bl, q4)])
                if CUT <= 4:
                    continue
                P.dma(SP, self.ynT[:, ts].rearrange("(kc p) t -> p kc t", p=128), ynT_t[b][:],
                      reads=[("synT", b, bl, q4) for bl in range(4) for q4 in range(4)], writes=[("ynT_d", tt)])

    def build(self):
        self.declare_io()
        self.final_ops = []
        with contextlib.ExitStack() as st:
            self.setup_consts(st)
            if self.plan is not None:
                for pi_, (what, i) in enumerate(self.plan):
                    src = self.xT_in if pi_ == 0 else self.xs
                    dst = self.yT_out if pi_ == len(self.plan) - 1 else self.xs
                    self.final_ops = []
                    if what == "ffn":
                        self.ffn(i, src, dst)
                    elif what == "attn":
                        self.attn_layer(i, i // 3, src, dst)
                    elif what == "gmlp":
                        self.gmlp_layer(i, src, dst)
                    elif what == "ssd":
                        self.ssd_layer(i, src, dst)
                self.P.emit(final_wait_ops=self.final_ops)
                return self.nc
            nl = len(self.layers)
            for li, i in enumerate(self.layers):
                first, last = (li == 0), (li == nl - 1)
                src = self.xT_in if first else self.xs
                kind = i % 3
                j = i // 3
                mid = self.xs
                if kind == 0:
                    self.attn_layer(i, j, src, mid)
                elif kind == 1:
                    self.gmlp_layer(i, src, mid)
                else:
                    self.ssd_layer(i, src, mid)
                self.final_ops = []
                self.ffn(i, mid, self.yT_out if last else self.xs)
            self.P.emit(final_wait_ops=self.final_ops)
        return self.nc


_CACHE = {}


def kernel(**inputs):
    x = np.asarray(inputs["x"], np.float32)
    cols, rows, wsT = pack_small(inputs)
    if "nc" not in _CACHE:
        plan = []
        for i in range(4):
            plan.append((("attn", "gmlp", "ssd")[i % 3], i))
            plan.append(("ffn", i))
        _CACHE["nc"] = Builder(plan=plan).build()
    nc = _CACHE["nc"]
    shared = {"cols": cols, "rows": rows, "wsT": wsT}
    for k in ("sb_w_qkv", "sb_w_o", "gm_w_in", "gm_w_out", "ssm_w_in", "ssm_w_out", "ffn_w_gu", "ffn_w_down"):
        shared[k] = np.ascontiguousarray(np.asarray(inputs[k], np.float32))
    in_maps = []
    for b in range(8):
        m = dict(shared)
        m["xT"] = np.ascontiguousarray(x[b].T)
        in_maps.append(m)
    res = run_bass_kernel_spmd(nc, in_maps, core_ids=list(range(8)))
    out = np.stack([np.ascontiguousarray(r["yT"].T) for r in res.results], axis=0)
    return out.astype(np.float32)
```
